# Optimizing a Trainium2 kernel written in Bass

```python
import math
import jax, jax.numpy as jnp
from jax import lax
import numpy as np

D_MODEL = 2048
BATCH = 4
SEQ = 2048
DEPTH = 2

N_MIXERS = 2
ATTN_HEAD_DIM = 64
ATTN_Q_HEADS = D_MODEL // ATTN_HEAD_DIM
ATTN_KV_HEADS = 8
ATTN_GROUP = ATTN_Q_HEADS // ATTN_KV_HEADS
WINDOW = 128
ATTN_BLOCK = 128
ROPE_THETA = 10000.0
ATTN_QKV_W = (ATTN_Q_HEADS + 2 * ATTN_KV_HEADS) * ATTN_HEAD_DIM
MLSTM_HEADS = 4
MLSTM_DV = D_MODEL // MLSTM_HEADS
MLSTM_DK = MLSTM_DV // 2
MLSTM_CHUNK = 64
MLSTM_QK_W = MLSTM_HEADS * MLSTM_DK
MLSTM_V_W = MLSTM_HEADS * MLSTM_DV
MLSTM_IN_W = 2 * MLSTM_QK_W + 2 * MLSTM_V_W + 4 * MLSTM_HEADS
D_FF = 4 * D_MODEL
DEEPNORM_ALPHA = (2.0 * DEPTH) ** 0.25
DEEPNORM_BETA = (8.0 * DEPTH) ** -0.25
LN_EPS = 1e-5
HEAD_NORM_EPS = 1e-6

kernel_name = 'hybrid_swa_sink_mlstm_deepnorm_adaln'


def layer_norm(x, g, b):
    xf = x.astype(jnp.float32)
    mu = jnp.mean(xf, axis=-1, keepdims=True)
    var = jnp.mean(jnp.square(xf - mu), axis=-1, keepdims=True)
    y = (xf - mu) * lax.rsqrt(var + LN_EPS)
    return (y * g.astype(jnp.float32) + b.astype(jnp.float32)).astype(x.dtype)


def rope_cos_sin(positions):
    inv_freq = 1.0 / (ROPE_THETA ** (jnp.arange(0, ATTN_HEAD_DIM, 2, dtype=jnp.float32) / ATTN_HEAD_DIM))
    ang = positions.astype(jnp.float32)[..., None] * inv_freq
    ang = jnp.concatenate([ang, ang], axis=-1)
    return jnp.cos(ang), jnp.sin(ang)


def apply_rope(t, cos, sin):
    t1, t2 = jnp.split(t, 2, axis=-1)
    rot = jnp.concatenate([-t2, t1], axis=-1)
    return (t * cos + rot * sin).astype(t.dtype)


def windowed_gqa_sink(h, positions, w_qkv, b_qkv, sink, w_o, b_o):
    B, S, _ = h.shape
    nb = S // ATTN_BLOCK
    proj = h @ w_qkv + b_qkv
    q, k, v = jnp.split(proj, [ATTN_Q_HEADS * ATTN_HEAD_DIM,
                               (ATTN_Q_HEADS + ATTN_KV_HEADS) * ATTN_HEAD_DIM], axis=-1)
    q = q.reshape(B, S, ATTN_KV_HEADS, ATTN_GROUP, ATTN_HEAD_DIM)
    k = k.reshape(B, S, ATTN_KV_HEADS, ATTN_HEAD_DIM)
    v = v.reshape(B, S, ATTN_KV_HEADS, ATTN_HEAD_DIM)
    cos, sin = rope_cos_sin(positions)
    q = apply_rope(q, cos[:, :, None, None, :], sin[:, :, None, None, :])
    k = apply_rope(k, cos[:, :, None, :], sin[:, :, None, :])
    qb = q.reshape(B, nb, ATTN_BLOCK, ATTN_KV_HEADS, ATTN_GROUP, ATTN_HEAD_DIM)
    pad = ((0, 0), (ATTN_BLOCK, ATTN_BLOCK), (0, 0), (0, 0))
    kp = jnp.pad(k, pad).reshape(B, nb + 2, ATTN_BLOCK, ATTN_KV_HEADS, ATTN_HEAD_DIM)
    vp = jnp.pad(v, pad).reshape(B, nb + 2, ATTN_BLOCK, ATTN_KV_HEADS, ATTN_HEAD_DIM)
    kw = jnp.concatenate([kp[:, :-2], kp[:, 1:-1], kp[:, 2:]], axis=2)
    vw = jnp.concatenate([vp[:, :-2], vp[:, 1:-1], vp[:, 2:]], axis=2)
    s = jnp.einsum('bnqhgd,bnkhd->bnhgqk', qb, kw).astype(jnp.float32) * (ATTN_HEAD_DIM ** -0.5)
    blk = jnp.arange(nb)[:, None, None] * ATTN_BLOCK
    qpos = blk + jnp.arange(ATTN_BLOCK)[None, :, None]
    kpos = blk - ATTN_BLOCK + jnp.arange(3 * ATTN_BLOCK)[None, None, :]
    valid = (jnp.abs(kpos - qpos) <= WINDOW) & (kpos >= 0) & (kpos < S)
    s = jnp.where(valid[None, :, None, None], s, -jnp.inf)
    sk = sink.astype(jnp.float32).reshape(1, 1, ATTN_KV_HEADS, ATTN_GROUP, 1, 1)
    m = jnp.maximum(jnp.max(s, axis=-1, keepdims=True), sk)
    p = jnp.exp(s - m)
    p = p / (jnp.sum(p, axis=-1, keepdims=True) + jnp.exp(sk - m))
    o = jnp.einsum('bnhgqk,bnkhd->bnqhgd', p.astype(vw.dtype), vw)
    o = o.reshape(B, S, ATTN_Q_HEADS * ATTN_HEAD_DIM)
    return o @ w_o + b_o


def mlstm_chunkwise(q, k, v, log_i, log_f):
    Z, S, H, _ = q.shape
    L = MLSTM_CHUNK
    nc = S // L
    def to_chunks(t):
        t = t.reshape((Z, nc, L) + t.shape[2:])
        return jnp.moveaxis(jnp.swapaxes(t, 2, 3), 1, 0)
    qc, kc, vc = to_chunks(q), to_chunks(k), to_chunks(v)
    lic, lfc = to_chunks(log_i), to_chunks(log_f)
    tril = jnp.tril(jnp.ones((L, L), dtype=bool))

    def body(carry, inp):
        C, n, m = carry
        qq, kk, vv, li, lf = inp
        g = jnp.cumsum(lf, axis=-1)
        G = g[..., -1]
        a = g + m[..., None]
        Dm = g[..., :, None] - g[..., None, :] + li[..., None, :]
        Dm = jnp.where(tril, Dm, -jnp.inf)
        m_t = jnp.maximum(a, jnp.max(Dm, axis=-1))
        ea = jnp.exp(a - m_t)
        sc = jnp.einsum('zhtd,zhsd->zhts', qq, kk) * jnp.exp(Dm - m_t[..., None])
        num = ea[..., None] * jnp.einsum('zhtd,zhdv->zhtv', qq, C) + jnp.einsum('zhts,zhsv->zhtv', sc, vv)
        den = ea * jnp.einsum('zhtd,zhd->zht', qq, n) + jnp.sum(sc, axis=-1)
        hh = num / jnp.maximum(jnp.abs(den), jnp.exp(-m_t))[..., None]
        w_log = G[..., None] - g + li
        m_new = jnp.maximum(G + m, jnp.max(w_log, axis=-1))
        decay = jnp.exp(G + m - m_new)
        w = jnp.exp(w_log - m_new[..., None])
        C_new = decay[..., None, None] * C + jnp.einsum('zhs,zhsd,zhsv->zhdv', w, kk, vv)
        n_new = decay[..., None] * n + jnp.einsum('zhs,zhsd->zhd', w, kk)
        return (C_new, n_new, m_new), hh

    init = (jnp.zeros((Z, H, MLSTM_DK, MLSTM_DV), jnp.float32),
            jnp.zeros((Z, H, MLSTM_DK), jnp.float32),
            jnp.full((Z, H), -1e30, jnp.float32))
    _, hs = lax.scan(body, init, (qc, kc, vc, lic, lfc))
    hs = jnp.swapaxes(jnp.moveaxis(hs, 0, 1), 2, 3)
    return hs.reshape(Z, S, H, MLSTM_DV)


def bidir_mlstm(h, w_in, b_in, norm_w, w_o, b_o):
    B, S, _ = h.shape
    proj = h @ w_in + b_in
    q, k, v, o, gates = jnp.split(proj, [MLSTM_QK_W, 2 * MLSTM_QK_W, 2 * MLSTM_QK_W + MLSTM_V_W,
                                         2 * MLSTM_QK_W + 2 * MLSTM_V_W], axis=-1)
    f32 = jnp.float32
    q = q.astype(f32).reshape(B, S, MLSTM_HEADS, MLSTM_DK) * (MLSTM_DK ** -0.5)
    k = k.astype(f32).reshape(B, S, MLSTM_HEADS, MLSTM_DK)
    v = v.astype(f32).reshape(B, S, MLSTM_HEADS, MLSTM_DV)
    gates = gates.astype(f32).reshape(B, S, 4, MLSTM_HEADS)
    both = lambda fw, bw: jnp.concatenate([fw, jnp.flip(bw, axis=1)], axis=0)
    log_i = both(gates[:, :, 0], gates[:, :, 2])
    log_f = jax.nn.log_sigmoid(both(gates[:, :, 1], gates[:, :, 3]))
    hz = mlstm_chunkwise(both(q, q), both(k, k), both(v, v), log_i, log_f)
    hsum = hz[:B] + jnp.flip(hz[B:], axis=1)
    mu = jnp.mean(hsum, axis=-1, keepdims=True)
    var = jnp.mean(jnp.square(hsum - mu), axis=-1, keepdims=True)
    hn = (hsum - mu) * lax.rsqrt(var + HEAD_NORM_EPS)
    hn = hn * norm_w.astype(f32).reshape(MLSTM_HEADS, MLSTM_DV)
    y = jax.nn.sigmoid(o.astype(f32)) * hn.reshape(B, S, MLSTM_V_W)
    return y.astype(h.dtype) @ w_o + b_o


def sq_relu_mlp(h, w1, b1, w2, b2):
    u = jnp.square(jax.nn.relu(h @ w1 + b1))
    return u @ w2 + b2


def setup_inputs(seed: int = 0) -> dict:
    key = jax.random.key(seed)
    ks = jax.random.split(key, 24)
    n_attn = (DEPTH + 1) // N_MIXERS
    n_ml = DEPTH // N_MIXERS
    f32 = jnp.float32
    def w(k, shape, fan_in, gain=1.0):
        return jax.random.normal(k, shape, f32) * (gain * fan_in ** -0.5)
    def small(k, shape, s=0.02):
        return jax.random.normal(k, shape, f32) * s
    x = jax.random.normal(ks[0], (BATCH, SEQ, D_MODEL), f32)
    c = jax.random.normal(ks[1], (BATCH, D_MODEL), f32)
    positions = jnp.broadcast_to(jnp.arange(SEQ, dtype=jnp.int32)[None, :], (BATCH, SEQ))
    attn_w_qkv = w(ks[2], (n_attn, D_MODEL, ATTN_QKV_W), D_MODEL)
    attn_b_qkv = small(ks[3], (n_attn, ATTN_QKV_W))
    attn_sink = jax.random.normal(ks[4], (n_attn, ATTN_Q_HEADS), f32)
    attn_w_o = w(ks[5], (n_attn, ATTN_Q_HEADS * ATTN_HEAD_DIM, D_MODEL), ATTN_Q_HEADS * ATTN_HEAD_DIM, DEEPNORM_BETA)
    attn_b_o = small(ks[6], (n_attn, D_MODEL))
    mlstm_w_in = w(ks[7], (n_ml, D_MODEL, MLSTM_IN_W), D_MODEL)
    f_bias = jnp.linspace(3.0, 6.0, MLSTM_HEADS, dtype=f32)
    gate_bias = jnp.zeros((4, MLSTM_HEADS), f32).at[1].set(f_bias).at[3].set(f_bias)
    base_bias = jnp.concatenate([jnp.zeros((MLSTM_IN_W - 4 * MLSTM_HEADS,), f32), gate_bias.reshape(-1)])
    mlstm_b_in = small(ks[8], (n_ml, MLSTM_IN_W), 0.1) + base_bias
    mlstm_norm_w = 1.0 + small(ks[9], (n_ml, MLSTM_V_W))
    mlstm_w_o = w(ks[10], (n_ml, MLSTM_V_W, D_MODEL), MLSTM_V_W, DEEPNORM_BETA)
    mlstm_b_o = small(ks[11], (n_ml, D_MODEL))
    mod_w = w(ks[12], (DEPTH, D_MODEL, 6 * D_MODEL), D_MODEL, 0.5)
    mod_b = small(ks[13], (DEPTH, 6 * D_MODEL))
    mlp_w1 = w(ks[14], (DEPTH, D_MODEL, D_FF), D_MODEL)
    mlp_b1 = small(ks[15], (DEPTH, D_FF))
    mlp_w2 = w(ks[16], (DEPTH, D_FF, D_MODEL), D_FF, DEEPNORM_BETA)
    mlp_b2 = small(ks[17], (DEPTH, D_MODEL))
    ln_mix_g = 1.0 + small(ks[18], (DEPTH, D_MODEL))
    ln_mix_b = small(ks[19], (DEPTH, D_MODEL))
    ln_mlp_g = 1.0 + small(ks[20], (DEPTH, D_MODEL))
    ln_mlp_b = small(ks[21], (DEPTH, D_MODEL))
    return {'x': x, 'c': c, 'positions': positions,
            'attn_w_qkv': attn_w_qkv, 'attn_b_qkv': attn_b_qkv, 'attn_sink': attn_sink,
            'attn_w_o': attn_w_o, 'attn_b_o': attn_b_o,
            'mlstm_w_in': mlstm_w_in, 'mlstm_b_in': mlstm_b_in, 'mlstm_norm_w': mlstm_norm_w,
            'mlstm_w_o': mlstm_w_o, 'mlstm_b_o': mlstm_b_o,
            'mod_w': mod_w, 'mod_b': mod_b,
            'mlp_w1': mlp_w1, 'mlp_b1': mlp_b1, 'mlp_w2': mlp_w2, 'mlp_b2': mlp_b2,
            'ln_mix_g': ln_mix_g, 'ln_mix_b': ln_mix_b, 'ln_mlp_g': ln_mlp_g, 'ln_mlp_b': ln_mlp_b}


def reference(x, c, positions, attn_w_qkv, attn_b_qkv, attn_sink, attn_w_o, attn_b_o,
              mlstm_w_in, mlstm_b_in, mlstm_norm_w, mlstm_w_o, mlstm_b_o,
              mod_w, mod_b, mlp_w1, mlp_b1, mlp_w2, mlp_b2,
              ln_mix_g, ln_mix_b, ln_mlp_g, ln_mlp_b):
    c_act = jax.nn.silu(c)
    for i in range(DEPTH):
        mod = c_act @ mod_w[i] + mod_b[i]
        sh_m, sc_m, g_m, sh_f, sc_f, g_f = jnp.split(mod, 6, axis=-1)
        hmix = x * (1.0 + sc_m[:, None, :]) + sh_m[:, None, :]
        j = i // N_MIXERS
        if i % N_MIXERS == 0:
            y = windowed_gqa_sink(hmix, positions, attn_w_qkv[j], attn_b_qkv[j], attn_sink[j],
                                  attn_w_o[j], attn_b_o[j])
        else:
            y = bidir_mlstm(hmix, mlstm_w_in[j], mlstm_b_in[j], mlstm_norm_w[j],
                            mlstm_w_o[j], mlstm_b_o[j])
        x = layer_norm(DEEPNORM_ALPHA * x + (1.0 + g_m[:, None, :]) * y, ln_mix_g[i], ln_mix_b[i])
        hff = x * (1.0 + sc_f[:, None, :]) + sh_f[:, None, :]
        y = sq_relu_mlp(hff, mlp_w1[i], mlp_b1[i], mlp_w2[i], mlp_b2[i])
        x = layer_norm(DEEPNORM_ALPHA * x + (1.0 + g_f[:, None, :]) * y, ln_mlp_g[i], ln_mlp_b[i])
    return x
```

```python
import numpy as np
import ml_dtypes
import concourse.bass as bass
import concourse.mybir as mybir
from concourse.bass_utils import run_bass_kernel_spmd

F32 = mybir.dt.float32
BF16 = mybir.dt.bfloat16
I32 = mybir.dt.int32
AF = mybir.ActivationFunctionType
ALU = mybir.AluOpType
AX = mybir.AxisListType

D = 2048
NT = 8
TOK = 1024
TOKH = 1152
ALPHA = 4.0 ** 0.25
LN_EPS = 1e-5
HN_EPS = 1e-6
PI = float(np.pi)


class KB:
    def __init__(self, nc):
        self.nc = nc
        self.eng = {'pe': nc.tensor, 'act': nc.scalar, 'dve': nc.vector, 'pool': nc.gpsimd, 'sp': nc.sync}
        self.sem = {e: nc.alloc_semaphore("s_" + e) for e in ['pe', 'act', 'dve', 'pool']}
        self.cnt = {e: 0 for e in self.sem}
        self.waited = {}
        self.res = {}
        self.dsem = {}
        self.dcnt = {}
        self.rr = {'sp': 0, 'pool': 0, 'act': 0}
        self.nrr = {'sp': 16, 'pool': 8, 'act': 2}
        for q, n in self.nrr.items():
            for i in range(n):
                self._mk_dsem("rr%s%d" % (q, i))

    def _mk_dsem(self, name):
        self.dsem[name] = self.nc.alloc_semaphore("d_" + name)
        self.dcnt[name] = 0

    def _semof(self, key):
        return self.sem[key] if key in self.sem else self.dsem[key]

    def _deps(self, reads, writes):
        deps = []
        for r in reads:
            st = self.res.get(r)
            if st and st['w']:
                deps.append(st['w'])
        for w in writes:
            st = self.res.get(w)
            if st:
                if st['w']:
                    deps.append(st['w'])
                deps.extend(st['r'])
        return deps

    def _wait(self, engine, deps):
        best = {}
        for (key, c) in deps:
            if key == engine and engine == 'pe':
                continue
            if best.get(key, 0) < c:
                best[key] = c
        for key, c in best.items():
            if self.waited.get((engine, key), 0) >= c:
                continue
            if key in self.cnt:
                assert c <= self.cnt[key], (engine, key, c, self.cnt[key])
            self.eng[engine].wait_ge(self._semof(key), c)
            self.waited[(engine, key)] = c

    def _mark(self, ev, reads, writes):
        for r in reads:
            st = self.res.setdefault(r, {'w': None, 'r': []})
            st['r'].append(ev)
            if len(st['r']) > 24:
                best = {}
                for (k, c) in st['r']:
                    best[k] = max(best.get(k, 0), c)
                st['r'] = list(best.items())
        for w in writes:
            self.res[w] = {'w': ev, 'r': []}

    def op(self, engine, fn, reads=(), writes=(), inc=True):
        self._wait(engine, self._deps(reads, writes))
        ins = fn(self.eng[engine])
        if inc:
            ins.then_inc(self.sem[engine], 1)
            self.cnt[engine] += 1
            ev = (engine, self.cnt[engine])
        else:
            assert engine == 'pe'
            ev = (engine, self.cnt[engine] + 1)
        self._mark(ev, reads, writes)
        return ins

    def dma(self, q, out, in_, reads=(), writes=(), sem=None, **kw):
        if sem is None:
            sem = "rr%s%d" % (q, self.rr[q])
            self.rr[q] = (self.rr[q] + 1) % self.nrr[q]
        if sem not in self.dsem:
            self._mk_dsem(sem)
        self._wait(q, self._deps(reads, writes))
        ins = self.eng[q].dma_start(out=out, in_=in_, **kw)
        ins.then_inc(self.dsem[sem], 16)
        self.dcnt[sem] += 16
        ev = (sem, self.dcnt[sem])
        self._mark(ev, reads, writes)
        return ev

    def barrier(self):
        for e in ['pe', 'act', 'dve', 'pool', 'sp']:
            for key in self.cnt:
                if key == e and e == 'pe':
                    continue
                c = self.cnt[key]
                if c > 0 and self.waited.get((e, key), 0) < c:
                    self.eng[e].wait_ge(self.sem[key], c)
                    self.waited[(e, key)] = c
            for key, c in self.dcnt.items():
                if c > 0 and self.waited.get((e, key), 0) < c:
                    self.eng[e].wait_ge(self.dsem[key], c)
                    self.waited[(e, key)] = c

    def wait_all_on(self, engine, names):
        deps = []
        for n in names:
            st = self.res.get(n)
            if st:
                if st['w']:
                    deps.append(st['w'])
                deps.extend(st['r'])
        self._wait(engine, deps)


def build(stop_after=None):
    nc = bass.Bass("TRN2", target_bir_lowering=False)
    kb = KB(nc)

    def din(name, shape, dt=F32):
        return nc.dram_tensor(name, list(shape), dt, kind="ExternalInput").ap()

    x_in = din("x_loc", [TOKH, D])
    pos_in = din("pos_loc", [1, TOKH], I32)
    cT_in = din("cT", [128, 16, 4])
    modw_in = din("modw", [D, 3072])
    modb_in = din("modb", [1, 3072])
    onehot_in = din("onehot", [4, 128])
    sel_in = din("sel", [128, 2])
    cst_in = din("cst", [128, 1024])
    wqkv_in = din("wqkv", [D, 3072])
    bqk_in = din("bqk", [128, 20])
    bv_in = din("bv", [1, 512])
    sink_in = din("sink", [1, 32])
    awo_in = din("awo", [D, D])
    abo_in = din("abo", [1, D])
    mwin_in = din("mwin", [D, 6144])
    mwg_in = din("mwg", [D, 16])
    mbqk_in = din("mbqk", [128, 16])
    mbvo_in = din("mbvo", [1, 4096])
    mbg_in = din("mbg", [1, 16])
    mnw_in = din("mnw", [1, D])
    mwo_in = din("mwo", [D, D])
    mbo_in = din("mbo", [1, D])
    w1_in = din("w1", [2, D, 4 * D])
    b1_in = din("b1c", [2, 128, 64])
    w2_in = din("w2", [2, 4 * D, D])
    b2_in = din("b2", [2, 1, D])
    lng_in = din("lng", [4, 1, D])
    lnb_in = din("lnb", [4, 1, D])
    y_out = nc.dram_tensor("y_out", [TOK, D], F32, kind="ExternalOutput").ap()

    ag_in = nc.dram_tensor("ag_in", [4, 3072], F32)
    ag_out = nc.dram_tensor("ag_out", [32, 3072], F32)
    mx_in = nc.dram_tensor("mx_in", [1, 8], F32)
    mx_out = nc.dram_tensor("mx_out", [2, 8], F32)
    st_in = [nc.dram_tensor("st_in%d" % h, [264, 512], F32) for h in range(4)]
    st_out = [nc.dram_tensor("st_out%d" % h, [528, 512], F32) for h in range(4)]

    def sb(name, shape, dt):
        return nc.alloc_sbuf_tensor(name, list(shape), dt)

    X = sb("X", [128, NT, D], F32)
    HT = sb("HT", [128, 16, TOKH], BF16)
    CST = sb("CST", [128, 1024], F32)
    IDB = sb("IDB", [128, 128], BF16)
    MGE = sb("MGE", [128, 512], BF16)
    MLE = sb("MLE", [128, 512], BF16)
    ONESB = sb("ONESB", [128, 128], BF16)
    ONESF = sb("ONESF", [128, 128], F32)
    SEL = sb("SEL", [128, 2], F32)
    OH = sb("OH", [4, 128], F32)
    WORK = sb("WORK", [128, 100 * 1024], mybir.dt.uint8)

    IDF = CST[:, 0:128]
    MGEF = CST[:, 128:256]
    MLEF = CST[:, 256:384]
    INVF = CST[:, 384:385]
    SGN = CST[:, 385:386]
    KM0 = CST[:, 386:387]
    KM1 = CST[:, 387:388]
    EYE4 = CST[0:4, 388:392]

    class Arena:
        def __init__(self):
            self.off = 0

        def reset(self, off=0):
            self.off = off

        def get(self, shape, dt):
            nbytes = int(np.prod(shape[1:])) * mybir.dt.size(dt)
            nbytes_al = (nbytes + 31) // 32 * 32
            assert self.off + nbytes_al <= 100 * 1024, (self.off, nbytes_al)
            v = WORK[0:shape[0], self.off:self.off + nbytes].bitcast(dt)
            self.off += nbytes_al
            if len(shape) > 2:
                names = " ".join("d%d" % i for i in range(1, len(shape)))
                kw = {"d%d" % i: shape[i] for i in range(1, len(shape))}
                v = v.rearrange("p (%s) -> p %s" % (names, names), **kw)
            return v

    ar = Arena()

    PS = [nc.alloc_psum_tensor("ps%d" % i, [128, 512], F32) for i in range(8)]

    def psname(i):
        return "ps%d" % i

    kb.dma('sp', CST[:], cst_in, writes=["CST"])
    kb.dma('sp', SEL[:], sel_in, writes=["SEL"])
    kb.dma('sp', OH[:], onehot_in, writes=["OH"])
    kb.op('dve', lambda e: e.tensor_copy(out=IDB[:], in_=IDF), reads=["CST"], writes=["IDB"])
    kb.op('dve', lambda e: e.tensor_copy(out=MGE[:].rearrange("p (a b) -> p a b", a=4),
                                         in_=MGEF.unsqueeze(1).to_broadcast([128, 4, 128])), reads=["CST"], writes=["MGE"])
    kb.op('dve', lambda e: e.tensor_copy(out=MLE[:].rearrange("p (a b) -> p a b", a=4),
                                         in_=MLEF.unsqueeze(1).to_broadcast([128, 4, 128])), reads=["CST"], writes=["MLE"])
    kb.op('pool', lambda e: e.memset(ONESB[:], 1.0), writes=["ONESB"])
    kb.op('pool', lambda e: e.memset(ONESF[:], 1.0), writes=["ONESF"])

    for t in range(NT):
        kb.dma('sp', X[:, t, :], x_in[t * 128:(t + 1) * 128, :], writes=["X%d" % t])


    SLOT_BYTES = 16 * 1024

    class Slots:
        def __init__(self, n):
            self.n = n
            self.raw = [ar.get([128, SLOT_BYTES // 2], BF16) for _ in range(n)]
            self.i = 0

        def load(self, src_ap, shape3):
            s = self.i % self.n
            self.i += 1
            rn = "slot%d" % s
            a, b = shape3
            v = self.raw[s][:, 0:a * b].rearrange("p (a b) -> p a b", a=a)
            kb.dma('pool', v, src_ap, writes=[rn], sem="w_" + rn)
            return v, rn

    def kview(ap2d):
        return ap2d.rearrange("(k p) n -> p k n", p=128)

    def bc_from_dram(dst, src_row, name, q='sp'):
        kb.dma(q, dst, src_row.partition_broadcast(128), writes=[name])

    def mod_bc(dst, layer, typ, name, add_one, tmp4):
        src = ag_out.ap()[:, layer * 1536 + typ * 256: layer * 1536 + typ * 256 + 256]
        src = src.rearrange("(r b) f -> b r f", b=4)
        kb.dma('sp', tmp4[:].rearrange("b (r f) -> b r f", r=8), src, reads=["ag_out"], writes=["tmp4"])
        for cc in range(4):
            bank = 4 + cc
            kb.op('pe', lambda e: e.matmul(PS[bank][:], lhsT=OH[:], rhs=tmp4[:, cc * 512:(cc + 1) * 512],
                                           start=True, stop=True), reads=["OH", "tmp4"], writes=[psname(bank)])
            if add_one:
                kb.op('dve', lambda e: e.tensor_scalar(out=dst[:, cc * 512:(cc + 1) * 512], in0=PS[bank][:],
                                                       scalar1=1.0, scalar2=None, op0=ALU.add),
                      reads=[psname(bank)], writes=[name])
            else:
                kb.op('dve', lambda e: e.tensor_copy(out=dst[:, cc * 512:(cc + 1) * 512], in_=PS[bank][:]),
                      reads=[psname(bank)], writes=[name])

    HT_ALL = ["HT_%d" % t for t in range(9)]

    def modulate_transpose(layer, tsc, tsh, ntile):
        kb.barrier()
        ar.reset(P0)
        sc1 = ar.get([128, D], F32)
        sh = ar.get([128, D], F32)
        tmp4 = ar.get([4, D], F32)
        tmpf = ar.get([128, D], F32)
        hb = [ar.get([128, D], BF16) for _ in range(2)]
        xh = None
        if ntile == 9:
            xh = ar.get([128, D], F32)
            kb.dma('sp', xh[:], x_in[1024:1152, :], writes=["xh"])
        mod_bc(sc1, layer, tsc, "sc1", True, tmp4)
        mod_bc(sh, layer, tsh, "sh", False, tmp4)
        for t in range(ntile):
            src = X[:, t, :] if t < NT else xh[:]
            srcn = "X%d" % t if t < NT else "xh"
            kb.op('dve', lambda e: e.tensor_tensor(out=tmpf[:], in0=src, in1=sc1[:], op=ALU.mult),
                  reads=[srcn, "sc1"], writes=["mt_tmpf"])
            hbt = hb[t % 2]
            hbn = "mt_hb%d" % (t % 2)
            kb.op('pool', lambda e: e.tensor_tensor(out=hbt[:], in0=tmpf[:], in1=sh[:], op=ALU.add),
                  reads=["mt_tmpf", "sh"], writes=[hbn])
            for half in range(2):
                bank = (2 * t + half) % 4
                pst = PS[bank][:].bitcast(BF16)
                for j in range(8):
                    c = half * 8 + j
                    kb.op('pe', lambda e: e.transpose(out=pst[:, j * 128:(j + 1) * 128], in_=hbt[:, c * 128:(c + 1) * 128],
                                                      identity=IDB[:]),
                          reads=[hbn, "IDB"], writes=[psname(bank)], inc=(j == 7))
                kb.op('act', lambda e: e.activation(out=HT[:, half * 8:(half + 1) * 8, t * 128:(t + 1) * 128],
                                                    in_=pst.rearrange("p (c n) -> p c n", c=8), func=AF.Copy),
                      reads=[psname(bank)], writes=["HT_%d" % t])
        kb.barrier()

    def prep_residual(layer, typ, bias_row_ap):
        kb.barrier()
        ar.reset(P0)
        tmp4 = ar.get([4, D], F32)
        tb = ar.get([128, D], F32)
        mod_bc(GATE, layer, typ, "gate", True, tmp4)
        bc_from_dram(tb[:], bias_row_ap, "tb")
        kb.op('dve', lambda e: e.tensor_tensor(out=tb[:], in0=tb[:], in1=GATE[:], op=ALU.mult), reads=["tb", "gate"], writes=["tb"])
        for t in range(NT):
            kb.op('act', lambda e: e.activation(out=X[:, t, :], in_=X[:, t, :], func=AF.Copy, scale=ALPHA),
                  reads=["X%d" % t], writes=["X%d" % t])
            kb.op('pool', lambda e: e.tensor_tensor(out=X[:, t, :], in0=X[:, t, :], in1=tb[:], op=ALU.add),
                  reads=["X%d" % t, "tb"], writes=["X%d" % t])
        kb.barrier()
        ar.reset(P0)

    pa_state = {"i": 0}

    def accum_block(tt, cc, emit_mm, banks=(6, 7)):
        bank = banks[pa_state["i"] % len(banks)]
        i2 = pa_state["i"] % 2
        pa_state["i"] += 1
        emit_mm(bank)
        tb = PATMP[i2]
        tn = "pa_tmp%d" % i2
        kb.op('dve', lambda e: e.tensor_tensor(out=tb[:], in0=PS[bank][:], in1=GATE[:, cc * 512:(cc + 1) * 512], op=ALU.mult),
              reads=[psname(bank), "gate"], writes=[tn])
        kb.op('pool', lambda e: e.tensor_tensor(out=X[:, tt, cc * 512:(cc + 1) * 512], in0=X[:, tt, cc * 512:(cc + 1) * 512],
                                                in1=tb[:], op=ALU.add),
              reads=[tn, "X%d" % tt], writes=["X%d" % tt])

    def proj_accum(lhs_view, lhs_names, w_rows_ap, banks=(4, 5, 6, 7)):
        wt, wn = slots.load(w_rows_ap.rearrange("(c p) n -> p c n", p=128), (4, D))
        for tt in range(NT):
            for cc in range(4):
                def mm(bank):
                    for c in range(4):
                        kb.op('pe', lambda e: e.matmul(PS[bank][:], lhsT=lhs_view(c)[:, tt * 128:(tt + 1) * 128],
                                                       rhs=wt[:, c, cc * 512:(cc + 1) * 512], start=(c == 0), stop=(c == 3)),
                              reads=lhs_names + [wn], writes=[psname(bank)], inc=(c == 3))
                accum_block(tt, cc, mm, banks)

    def layer_norm(lidx):
        kb.barrier()
        ar.reset(P0)
        lng = ar.get([128, D], F32)
        lnb = ar.get([128, D], F32)
        st = ar.get([128, 24], F32)
        mv = ar.get([128, 2], F32)
        rstd = ar.get([128, 1], F32)
        nb = ar.get([128, 1], F32)
        bc_from_dram(lng[:], lng_in[lidx], "lng")
        bc_from_dram(lnb[:], lnb_in[lidx], "lnb")
        for t in range(NT):
            xn = "X%d" % t
            for j in range(4):
                kb.op('dve', lambda e: e.bn_stats(out=st[:, j * 6:(j + 1) * 6], in_=X[:, t, j * 512:(j + 1) * 512]), reads=[xn], writes=["ln_st%d" % j])
            kb.op('dve', lambda e: e.bn_aggr(out=mv[:], in_=st[:]), reads=["ln_st%d" % j for j in range(4)], writes=["ln_mv"])
            kb.op('dve', lambda e: e.tensor_scalar(out=rstd[:], in0=mv[:, 1:2], scalar1=LN_EPS, scalar2=None, op0=ALU.add),
                  reads=["ln_mv"], writes=["ln_rstd"])
            kb.op('act', lambda e: e.activation(out=rstd[:], in_=rstd[:], func=AF.Sqrt), reads=["ln_rstd"], writes=["ln_rstd"])
            kb.op('dve', lambda e: e.reciprocal(out=rstd[:], in_=rstd[:]), reads=["ln_rstd"], writes=["ln_rstd"])
            kb.op('dve', lambda e: e.scalar_tensor_tensor(out=nb[:], in0=mv[:, 0:1], scalar=-1.0, in1=rstd[:], op0=ALU.mult, op1=ALU.mult),
                  reads=["ln_mv", "ln_rstd"], writes=["ln_nb"])
            kb.op('act', lambda e: e.activation(out=X[:, t, :], in_=X[:, t, :], func=AF.Identity, bias=nb[:], scale=rstd[:]),
                  reads=[xn, "ln_nb", "ln_rstd"], writes=[xn])
            kb.op('pool', lambda e: e.tensor_tensor(out=X[:, t, :], in0=X[:, t, :], in1=lng[:], op=ALU.mult),
                  reads=[xn, "lng"], writes=[xn])
            kb.op('dve', lambda e: e.tensor_tensor(out=X[:, t, :], in0=X[:, t, :], in1=lnb[:], op=ALU.add),
                  reads=[xn, "lnb"], writes=[xn])
        kb.barrier()

    def finish():
        kb.barrier()
        for t in range(NT):
            kb.dma('sp', y_out[t * 128:(t + 1) * 128, :], X[:, t, :], reads=["X%d" % t], writes=["yout%d" % t])
        kb.wait_all_on('sp', ["yout%d" % t for t in range(NT)])
        return nc

    ccsem = nc.alloc_semaphore("ccsem")
    cc_count = [0]
    kb.dsem["cc"] = ccsem
    kb.dcnt["cc"] = 0

    def collective(groups, src, dst, reads, writes):
        kb._wait('pool', kb._deps(reads, writes))
        nc.gpsimd.collective_compute("AllGather", ALU.bypass, replica_groups=groups,
                                     ins=[src.ap().opt()], outs=[dst.ap().opt()]).then_inc(ccsem)
        kb.dcnt["cc"] += 1
        kb._mark(("cc", kb.dcnt["cc"]), reads, writes)

    PAIRS = [[0, 1], [2, 3], [4, 5], [6, 7]]

    ar.reset(0)
    slots = Slots(2)
    S0 = ar.off
    GATE = ar.get([128, D], F32)
    PATMP = [ar.get([128, 512], F32) for _ in range(2)]
    P0 = ar.off

    cT = ar.get([128, 16, 4], F32)
    cA = ar.get([128, 16, 4], BF16)
    modsb = ar.get([4, 3072], F32)
    modb_bf = ar.get([1, 3072], BF16)
    kb.dma('sp', cT[:], cT_in, writes=["cT"])
    kb.dma('pool', modb_bf[:], modb_in, writes=["modb_bf"])
    kb.op('act', lambda e: e.activation(out=cA[:], in_=cT[:], func=AF.Silu), reads=["cT"], writes=["cA"])
    for pc in range(6):
        wt, wn = slots.load(kview(modw_in[:, pc * 512:(pc + 1) * 512]), (16, 512))
        bank = pc % 2
        for k in range(16):
            kb.op('pe', lambda e: e.matmul(PS[bank][0:4, :], lhsT=cA[:, k, :], rhs=wt[:, k, :], start=(k == 0), stop=False),
                  reads=["cA", wn], writes=[psname(bank)], inc=False)
        kb.op('pe', lambda e: e.matmul(PS[bank][0:4, :], lhsT=ONESB[0:1, 0:4], rhs=modb_bf[0:1, pc * 512:(pc + 1) * 512],
                                       start=False, stop=True), reads=["ONESB", "modb_bf"], writes=[psname(bank)])
        kb.op('dve', lambda e: e.tensor_copy(out=modsb[:, pc * 512:(pc + 1) * 512], in_=PS[bank][0:4, :]),
              reads=[psname(bank)], writes=["modsb"])
    kb.dma('pool', ag_in.ap(), modsb[:], reads=["modsb"], writes=["ag_in"])
    collective([list(range(8))], ag_in, ag_out, ["ag_in"], ["ag_out"])

    modulate_transpose(0, 1, 0, 9)

    if stop_after == "ht0":
        ar.reset(P0)
        dbg = ar.get([128, D], F32)
        for t in range(NT):
            kb.op('dve', lambda e: e.tensor_copy(out=dbg[:].rearrange("p (c n) -> p c n", c=16), in_=HT[:, :, t * 128:(t + 1) * 128]),
                  reads=HT_ALL, writes=["dbg"])
            kb.dma('sp', y_out[t * 128:(t + 1) * 128, :], dbg[:], reads=["dbg"], writes=["yout%d" % t])
        kb.wait_all_on('sp', ["yout%d" % t for t in range(NT)])
        return nc

    prep_residual(0, 2, abo_in)

    cosT = ar.get([128, TOKH], F32)
    sinS = ar.get([128, TOKH], F32)
    bqk = ar.get([128, 20], F32)
    bv_bf = ar.get([1, 512], BF16)
    sinkrow = ar.get([1, 32, 128], BF16)
    sinkf = ar.get([1, 32], F32)
    markA = ar.off
    angt = ar.get([128, TOKH], F32)
    posi = ar.get([128, TOKH], I32)
    kint = ar.get([128, TOKH], I32)
    kb.dma('sp', posi[:], pos_in.partition_broadcast(128), writes=["posi"])
    kb.dma('sp', bqk[:], bqk_in, writes=["bqk"])
    kb.dma('pool', bv_bf[:], bv_in, writes=["bv_bf"])
    kb.dma('sp', sinkf[:], sink_in, writes=["sinkf"])
    kb.op('dve', lambda e: e.tensor_copy(out=angt[:], in_=posi[:]), reads=["posi"], writes=["angt"])
    kb.op('dve', lambda e: e.tensor_scalar(out=angt[:], in0=angt[:], scalar1=INVF, scalar2=None, op0=ALU.mult),
          reads=["angt", "CST"], writes=["angt"])
    kb.op('dve', lambda e: e.tensor_scalar(out=cosT[:], in0=angt[:], scalar1=1.0 / (2 * PI), scalar2=None, op0=ALU.mult),
          reads=["angt"], writes=["cosT"])
    kb.op('dve', lambda e: e.tensor_copy(out=kint[:], in_=cosT[:]), reads=["cosT"], writes=["kint"])
    kb.op('dve', lambda e: e.tensor_copy(out=cosT[:], in_=kint[:]), reads=["kint"], writes=["cosT"])
    kb.op('dve', lambda e: e.scalar_tensor_tensor(out=angt[:], in0=cosT[:], scalar=-2 * PI, in1=angt[:], op0=ALU.mult, op1=ALU.add),
          reads=["cosT", "angt"], writes=["angt"])
    kb.op('dve', lambda e: e.tensor_scalar(out=cosT[:], in0=angt[:], scalar1=PI, scalar2=-2 * PI, op0=ALU.is_gt, op1=ALU.mult),
          reads=["angt"], writes=["cosT"])
    kb.op('dve', lambda e: e.tensor_tensor(out=sinS[:], in0=angt[:], in1=cosT[:], op=ALU.add), reads=["angt", "cosT"], writes=["sinS"])
    kb.op('dve', lambda e: e.tensor_scalar(out=angt[:], in0=angt[:], scalar1=PI / 2, scalar2=None, op0=ALU.add),
          reads=["angt"], writes=["angt"])
    kb.op('dve', lambda e: e.tensor_scalar(out=cosT[:], in0=angt[:], scalar1=PI, scalar2=-2 * PI, op0=ALU.is_gt, op1=ALU.mult),
          reads=["angt"], writes=["cosT"])
    kb.op('dve', lambda e: e.tensor_tensor(out=cosT[:], in0=angt[:], in1=cosT[:], op=ALU.add), reads=["angt", "cosT"], writes=["cosT"])
    kb.op('act', lambda e: e.activation(out=sinS[:], in_=sinS[:], func=AF.Sin), reads=["sinS"], writes=["sinS"])
    kb.op('act', lambda e: e.activation(out=cosT[:], in_=cosT[:], func=AF.Sin), reads=["cosT"], writes=["cosT"])
    kb.op('dve', lambda e: e.tensor_scalar(out=sinS[:], in0=sinS[:], scalar1=SGN, scalar2=None, op0=ALU.mult),
          reads=["sinS", "CST"], writes=["sinS"])
    kb.op('act', lambda e: e.activation(out=sinkf[:], in_=sinkf[:], func=AF.Exp), reads=["sinkf"], writes=["sinkf"])
    kb.op('dve', lambda e: e.tensor_copy(out=sinkrow[:], in_=sinkf[:].unsqueeze(2).to_broadcast([1, 32, 128])),
          reads=["sinkf"], writes=["sinkrow"])
    kb.barrier()
    ar.reset(markA)
    qT = ar.get([128, 4, TOK], BF16)
    kTz = [ar.get([128, TOKH], BF16) for _ in range(2)]
    vT = ar.get([128, 9, 128], BF16)
    oT = ar.get([128, 4, TOK], BF16)
    tq = ar.get([128, 512], F32)
    ta = ar.get([128, 512], F32)
    tb_ = ar.get([128, 512], F32)
    Pb = [ar.get([128, 3, 512], BF16) for _ in range(2)]
    rden = ar.get([128, 512], F32)
    ev_i = [0]

    def qk_chunk(wt, wn, lc, bcol, toks, dsts):
        t0, t1 = toks
        n = t1 - t0
        bank = ev_i[0] % 2
        ev_i[0] += 1
        for k in range(16):
            kb.op('pe', lambda e: e.matmul(PS[bank][:, 0:n], lhsT=wt[:, k, lc * 128:(lc + 1) * 128], rhs=HT[:, k, t0:t1],
                                           start=(k == 0), stop=(k == 15)),
                  reads=HT_ALL + [wn], writes=[psname(bank)], inc=(k == 15))
        kb.op('act', lambda e: e.activation(out=tq[:, 0:n], in_=PS[bank][:, 0:n], func=AF.Identity, bias=bqk[:, bcol:bcol + 1]),
              reads=[psname(bank), "bqk"], writes=["tq"])
        kb.op('dve', lambda e: e.tensor_tensor(out=ta[:, 0:n], in0=tq[:, 0:n], in1=cosT[:, t0:t1], op=ALU.mult),
              reads=["tq", "cosT"], writes=["ta"])
        kb.op('pool', lambda e: e.tensor_tensor(out=tb_[0:64, 0:n], in0=tq[64:128, 0:n], in1=sinS[64:128, t0:t1], op=ALU.mult),
              reads=["tq", "sinS"], writes=["tbl"])
        kb.op('pool', lambda e: e.tensor_tensor(out=tb_[64:128, 0:n], in0=tq[0:64, 0:n], in1=sinS[0:64, t0:t1], op=ALU.mult),
              reads=["tq", "sinS"], writes=["tbh"])
        if len(dsts) == 1:
            ap_, nm_, _ = dsts[0]
            kb.op('dve', lambda e: e.tensor_tensor(out=ap_, in0=ta[:, 0:n], in1=tb_[:, 0:n], op=ALU.add),
                  reads=["ta", "tbl", "tbh"], writes=[nm_])
        else:
            kb.op('dve', lambda e: e.tensor_tensor(out=ta[:, 0:n], in0=ta[:, 0:n], in1=tb_[:, 0:n], op=ALU.add),
                  reads=["ta", "tbl", "tbh"], writes=["ta"])
            for (ap_, nm_, mcol) in dsts:
                kb.op('pool', lambda e: e.tensor_scalar(out=ap_, in0=ta[:, 0:n], scalar1=mcol, scalar2=None, op0=ALU.mult),
                      reads=["ta", "CST"], writes=[nm_])

    unit = 0
    for m in range(4):
        cb = m * 768
        wt, wn = slots.load(kview(wqkv_in[:, cb:cb + 512]), (16, 512))
        for lc in range(4):
            for th in range(2):
                qk_chunk(wt, wn, lc, 4 * m + lc, (th * 512, (th + 1) * 512), [(qT[:, lc, th * 512:(th + 1) * 512], "qT", None)])
        wt, wn = slots.load(kview(wqkv_in[:, cb + 512:cb + 768]), (16, 256))
        for (t0, t1) in [(0, 512), (512, 1024), (1024, 1152)]:
            qk_chunk(wt, wn, 0, 16 + m, (t0, t1), [(kTz[0][:, t0:t1], "kTz0", KM0), (kTz[1][:, t0:t1], "kTz1", KM1)])
        for t in range(9):
            bank = 2 + (t % 2)
            for k in range(16):
                kb.op('pe', lambda e: e.matmul(PS[bank][:, 0:128], lhsT=HT[:, k, t * 128:(t + 1) * 128], rhs=wt[:, k, 128:256],
                                               start=(k == 0), stop=False),
                      reads=HT_ALL + [wn], writes=[psname(bank)], inc=False)
            kb.op('pe', lambda e: e.matmul(PS[bank][:, 0:128], lhsT=ONESB[0:1, :], rhs=bv_bf[0:1, m * 128:(m + 1) * 128], start=False, stop=True),
                  reads=["ONESB", "bv_bf"], writes=[psname(bank)])
            kb.op('act', lambda e: e.activation(out=vT[:, t, :], in_=PS[bank][:, 0:128], func=AF.Copy), reads=[psname(bank)], writes=["vT"])
        for n in range(8):
            for e_ in range(2):
                h = 2 * m + e_
                js = [j for j in range(3) if n + j - 1 >= 0]
                pb = Pb[unit % 2]
                pbn = "Pb%d" % (unit % 2)
                unit += 1
                for j in js:
                    kt = n + j - 1
                    bank = 2 + j
                    kb.op('pe', lambda e: e.matmul(PS[bank][:], lhsT=kTz[e_][:, kt * 128:(kt + 1) * 128],
                                                   rhs=qT[:, :, n * 128:(n + 1) * 128], start=True, stop=True),
                          reads=["qT", "kTz%d" % e_], writes=[psname(bank)])
                    kb.op('act', lambda e: e.activation(out=pb[:, j, :], in_=PS[bank][:], func=AF.Exp, scale=0.125),
                          reads=[psname(bank)], writes=[pbn + "_%d" % j])
                    if j == 0:
                        kb.op('pool', lambda e: e.tensor_tensor(out=pb[:, j, :], in0=pb[:, j, :], in1=MGE[:], op=ALU.mult),
                              reads=[pbn + "_%d" % j, "MGE"], writes=[pbn + "_%d" % j])
                    if j == 2:
                        kb.op('pool', lambda e: e.tensor_tensor(out=pb[:, j, :], in0=pb[:, j, :], in1=MLE[:], op=ALU.mult),
                              reads=[pbn + "_%d" % j, "MLE"], writes=[pbn + "_%d" % j])
                for idx, j in enumerate(js):
                    kt = n + j - 1
                    kb.op('pe', lambda e: e.matmul(PS[0][64 * e_:64 * e_ + 64, :], lhsT=vT[:, kt, e_ * 64:(e_ + 1) * 64], rhs=pb[:, j, :],
                                                   start=(idx == 0), stop=(idx == len(js) - 1)),
                          reads=["vT", pbn + "_%d" % j], writes=["ps0"], inc=(idx == len(js) - 1))
                for idx, j in enumerate(js):
                    kb.op('pe', lambda e: e.matmul(PS[1][64 * e_:64 * e_ + 64, :], lhsT=ONESB[:, 0:64], rhs=pb[:, j, :],
                                                   start=(idx == 0), stop=False),
                          reads=["ONESB", pbn + "_%d" % j], writes=["ps1"], inc=False)
                kb.op('pe', lambda e: e.matmul(PS[1][64 * e_:64 * e_ + 64, :], lhsT=ONESB[0:1, 0:64],
                                               rhs=sinkrow[0:1, 4 * h:4 * h + 4, :], start=False, stop=True),
                      reads=["ONESB", "sinkrow"], writes=["ps1"])
            kb.op('dve', lambda e: e.reciprocal(out=rden[:], in_=PS[1][:]), reads=["ps1"], writes=["rden"])
            kb.op('dve', lambda e: e.tensor_tensor(out=oT[:, :, n * 128:(n + 1) * 128],
                                                   in0=PS[0][:].rearrange("p (g q) -> p g q", g=4),
                                                   in1=rden[:].rearrange("p (g q) -> p g q", g=4), op=ALU.mult),
                  reads=["ps0", "rden"], writes=["oT"])
        proj_accum(lambda c: oT[:, c, :], ["oT"], awo_in[m * 512:(m + 1) * 512, :], banks=(4, 5, 6, 7))

    layer_norm(0)
    if stop_after == "attn":
        return finish()

    def mlp(layer, lidx):
        modulate_transpose(layer, 4, 3, NT)
        prep_residual(layer, 5, b2_in[layer])
        slot3 = ar.get([128, SLOT_BYTES // 2], BF16)
        slots.raw.append(slot3)
        slots.n = 3
        uT = [ar.get([128, 4, TOK], BF16) for _ in range(2)]
        b1c = ar.get([128, 64], F32)
        rl = [ar.get([128, 512], F32) for _ in range(2)]
        kb.dma('sp', b1c[:], b1_in[layer], writes=["b1c"])
        ai = 0
        for g in range(16):
            wt, wn = slots.load(kview(w1_in[layer][:, g * 512:(g + 1) * 512]), (16, 512))
            u = uT[g % 2]
            un = "uT%d" % (g % 2)
            for lc in range(4):
                for th in range(2):
                    bank = ai % 4
                    r_ = rl[ai % 2]
                    rn = "rl%d" % (ai % 2)
                    ai += 1
                    for k in range(16):
                        kb.op('pe', lambda e: e.matmul(PS[bank][:], lhsT=wt[:, k, lc * 128:(lc + 1) * 128], rhs=HT[:, k, th * 512:(th + 1) * 512],
                                                       start=(k == 0), stop=(k == 15)),
                              reads=HT_ALL + [wn], writes=[psname(bank)], inc=(k == 15))
                    kb.op('act', lambda e: e.activation(out=r_[:], in_=PS[bank][:], func=AF.Relu, bias=b1c[:, g * 4 + lc:g * 4 + lc + 1]),
                          reads=[psname(bank), "b1c"], writes=[rn])
                    kb.op('pool', lambda e: e.tensor_tensor(out=u[:, lc, th * 512:(th + 1) * 512], in0=r_[:], in1=r_[:], op=ALU.mult),
                          reads=[rn], writes=[un])
            proj_accum(lambda c: u[:, c, :], [un], w2_in[layer][g * 512:(g + 1) * 512, :], banks=(4, 5, 6, 7))
        kb.barrier()
        slots.raw.pop()
        slots.n = 2
        slots.i = 0
        layer_norm(lidx)

    mlp(0, 1)
    if stop_after == "mlp0":
        return finish()

    modulate_transpose(1, 1, 0, NT)
    prep_residual(1, 2, mbo_in)
    wg = ar.get([128, 16, 16], BF16)
    gt = ar.get([128, NT, 16], F32)
    bg = ar.get([128, 16], F32)
    lf = ar.get([128, NT, 2, 4], F32)
    gg = ar.get([128, NT, 2, 4], F32)
    dd = ar.get([128, NT, 2, 4], F32)
    uu = ar.get([128, NT, 2, 4], F32)
    fl = ar.get([128, NT, 2, 4], F32)
    al = ar.get([128, 2, 4, NT], F32)
    mub = ar.get([128, 2, 4, NT], F32)
    mib = ar.get([128, 2, 4, NT], F32)
    Amax = ar.get([4, 2, NT], F32)
    Gsum = ar.get([4, 2, NT], F32)
    mu_s = ar.get([4, 2, NT], F32)
    mi_s = ar.get([4, 2, NT], F32)
    mA = ar.get([4, NT + 1], F32)
    mB = ar.get([4, NT + 1], F32)
    mrecv = ar.get([4, 2, 2], F32)
    mxs = ar.get([4, 2], F32)
    dg = ar.get([4, 2, 4, NT], F32)
    bmqk = ar.get([128, 16], F32)
    kb.dma('pool', wg[:], kview(mwg_in), writes=["wg"])
    bc_from_dram(bg[:], mbg_in, "bg")
    kb.dma('sp', bmqk[:], mbqk_in, writes=["bmqk"])
    for t in range(NT):
        for k in range(16):
            kb.op('pe', lambda e: e.matmul(PS[0][:, t * 16:(t + 1) * 16], lhsT=HT[:, k, t * 128:(t + 1) * 128], rhs=wg[:, k, :],
                                           start=(k == 0), stop=(k == 15)),
                  reads=HT_ALL + ["wg"], writes=["ps0"], inc=(k == 15))
    kb.op('dve', lambda e: e.tensor_tensor(out=gt[:], in0=PS[0][:, 0:128].rearrange("p (t c) -> p t c", t=NT),
                                           in1=bg[:].unsqueeze(1).to_broadcast([128, NT, 16]), op=ALU.add),
          reads=["ps0", "bg"], writes=["gt"])
    gt4 = gt[:].rearrange("p t (d i h) -> p t d i h", d=2, i=2)
    kb.op('act', lambda e: e.activation(out=lf[:], in_=gt4[:, :, :, 1, :], func=AF.Exp, scale=-1.0), reads=["gt"], writes=["lf"])
    kb.op('act', lambda e: e.activation(out=lf[:], in_=lf[:], func=AF.Ln, bias=1.0), reads=["lf"], writes=["lf"])
    kb.op('dve', lambda e: e.tensor_scalar(out=lf[:], in0=lf[:], scalar1=-1.0, scalar2=None, op0=ALU.mult), reads=["lf"], writes=["lf"])
    for t in range(NT):
        kb.op('pe', lambda e: e.matmul(PS[1][:, t * 8:t * 8 + 4], lhsT=MLEF, rhs=lf[:, t, 0, :], start=True, stop=True),
              reads=["CST", "lf"], writes=["ps1"], inc=False)
        kb.op('pe', lambda e: e.matmul(PS[1][:, t * 8 + 4:t * 8 + 8], lhsT=MGEF, rhs=lf[:, t, 1, :], start=True, stop=True),
              reads=["CST", "lf"], writes=["ps1"], inc=(t == NT - 1))
    kb.op('dve', lambda e: e.tensor_copy(out=gg[:], in_=PS[1][:, 0:64].rearrange("p (t d h) -> p t d h", t=NT, d=2)),
          reads=["ps1"], writes=["gg"])
    kb.op('dve', lambda e: e.tensor_tensor(out=dd[:], in0=gt4[:, :, :, 0, :], in1=gg[:], op=ALU.subtract), reads=["gt", "gg"], writes=["dd"])
    for d_ in range(2):
        for tq_ in range(2):
            bank = 2 + d_ * 2 + tq_
            for j in range(4):
                t = tq_ * 4 + j
                kb.op('pe', lambda e: e.matmul(PS[bank][0:4, j * 128:(j + 1) * 128], lhsT=dd[:, t, d_, :], rhs=IDF, start=True, stop=True),
                      reads=["dd", "CST"], writes=[psname(bank)], inc=(j == 3))
            kb.op('dve', lambda e: e.tensor_reduce(out=Amax[:, d_, tq_ * 4:(tq_ + 1) * 4],
                                                   in_=PS[bank][0:4, :].rearrange("p (j n) -> p j n", j=4), axis=AX.X, op=ALU.max),
                  reads=[psname(bank)], writes=["Amax"])
        for t in range(NT):
            kb.op('pe', lambda e: e.matmul(PS[6][0:4, d_ * NT + t:d_ * NT + t + 1], lhsT=lf[:, t, d_, :], rhs=ONESF[:, 0:1], start=True, stop=True),
                  reads=["lf", "ONESF"], writes=["ps6"], inc=(t == NT - 1))
    kb.op('dve', lambda e: e.tensor_copy(out=Gsum[:], in_=PS[6][0:4, 0:16].rearrange("p (d t) -> p d t", d=2)), reads=["ps6"], writes=["Gsum"])
    kb.op('dve', lambda e: e.memset(mA[:, 0:1], -1.0e30), writes=["mA"])
    for t in range(NT):
        kb.op('dve', lambda e: e.tensor_tensor(out=mu_s[:, 0, t:t + 1], in0=mA[:, t:t + 1], in1=Amax[:, 0, t:t + 1], op=ALU.max),
              reads=["mA", "Amax"], writes=["mu_s"])
        kb.op('dve', lambda e: e.tensor_tensor(out=mA[:, t + 1:t + 2], in0=Gsum[:, 0, t:t + 1], in1=mu_s[:, 0, t:t + 1], op=ALU.add),
              reads=["Gsum", "mu_s"], writes=["mA"])
    kb.op('dve', lambda e: e.tensor_copy(out=mi_s[:, 0, :], in_=mA[:, 0:NT]), reads=["mA"], writes=["mi_s"])
    kb.op('dve', lambda e: e.memset(mxs[:], 0.0), writes=["mxs"])
    kb.op('dve', lambda e: e.tensor_copy(out=mxs[:, 0:1], in_=mA[:, NT:NT + 1]), reads=["mA", "mxs"], writes=["mxs"])
    kb.dma('sp', mx_in.ap()[0:1, :].rearrange("o (h j) -> (o h) j", j=2), mxs[:], reads=["mxs"], writes=["mx_in"])
    collective(PAIRS, mx_in, mx_out, ["mx_in"], ["mx_out"])
    with nc.allow_non_contiguous_dma(reason="tiny scalar exchange"):
        kb.dma('sp', mrecv[:], mx_out.ap()[0:2, :].rearrange("s (h j) -> h s j", j=2), reads=["mx_out"], writes=["mrecv"])
    kb.op('dve', lambda e: e.tensor_scalar(out=mB[:, NT:NT + 1], in0=mrecv[:, 0, 0:1], scalar1=SEL[0:4, 0:1], scalar2=None, op0=ALU.mult),
          reads=["mrecv", "SEL"], writes=["mB"])
    kb.op('dve', lambda e: e.scalar_tensor_tensor(out=mB[:, NT:NT + 1], in0=mrecv[:, 1, 0:1], scalar=SEL[0:4, 1:2], in1=mB[:, NT:NT + 1],
                                                  op0=ALU.mult, op1=ALU.add), reads=["mrecv", "SEL", "mB"], writes=["mB"])
    for t in range(NT - 1, -1, -1):
        kb.op('dve', lambda e: e.tensor_tensor(out=mu_s[:, 1, t:t + 1], in0=mB[:, t + 1:t + 2], in1=Amax[:, 1, t:t + 1], op=ALU.max),
              reads=["mB", "Amax"], writes=["mu_s"])
        kb.op('dve', lambda e: e.tensor_tensor(out=mB[:, t:t + 1], in0=Gsum[:, 1, t:t + 1], in1=mu_s[:, 1, t:t + 1], op=ALU.add),
              reads=["Gsum", "mu_s"], writes=["mB"])
    kb.op('dve', lambda e: e.tensor_copy(out=mi_s[:, 1, :], in_=mB[:, 1:NT + 1]), reads=["mB"], writes=["mi_s"])
    for (src, srcn, dst, dstn) in [(mu_s, "mu_s", mub, "mub"), (mi_s, "mi_s", mib, "mib")]:
        kb.op('dve', lambda e: e.tensor_tensor(out=dg[:], in0=src[:].unsqueeze(2).to_broadcast([4, 2, 4, NT]),
                                               in1=EYE4.unsqueeze(1).unsqueeze(3).to_broadcast([4, 2, 4, NT]), op=ALU.mult),
              reads=[srcn, "CST"], writes=["dg"])
        kb.op('pe', lambda e: e.matmul(PS[7][:, 0:64], lhsT=ONESF[0:4, :], rhs=dg[:].rearrange("p d h t -> p (d h t)"), start=True, stop=True),
              reads=["ONESF", "dg"], writes=["ps7"])
        kb.op('dve', lambda e: e.tensor_copy(out=dst[:].rearrange("p d h t -> p (d h t)"), in_=PS[7][:, 0:64]), reads=["ps7"], writes=[dstn])
    kb.op('dve', lambda e: e.tensor_tensor(out=al[:], in0=mib[:], in1=mub[:], op=ALU.subtract), reads=["mib", "mub"], writes=["al"])
    kb.op('act', lambda e: e.activation(out=al[:], in_=al[:], func=AF.Exp), reads=["al"], writes=["al"])
    mubT = mub[:].rearrange("p d h t -> p t d h")
    kb.op('dve', lambda e: e.tensor_tensor(out=uu[:], in0=dd[:], in1=mubT, op=ALU.subtract), reads=["dd", "mub"], writes=["uu"])
    kb.op('act', lambda e: e.activation(out=uu[:], in_=uu[:], func=AF.Exp), reads=["uu"], writes=["uu"])
    kb.op('dve', lambda e: e.scalar_tensor_tensor(out=fl[:], in0=gg[:], scalar=-1.0, in1=mubT, op0=ALU.mult, op1=ALU.subtract),
          reads=["gg", "mub"], writes=["fl"])
    kb.op('act', lambda e: e.activation(out=fl[:], in_=fl[:], func=AF.Exp), reads=["fl"], writes=["fl"])

    if stop_after == "gates":
        for i_, (src_, n_) in enumerate([(uu, "uu"), (fl, "fl"), (al, "al"), (gg, "gg"), (dd, "dd")]):
            kb.op('dve', lambda e: e.tensor_copy(out=X[:, 0, i_ * 64:(i_ + 1) * 64], in_=src_[:].rearrange("p a b c -> p (a b c)")),
                  reads=[n_, "X0"], writes=["X0"])
        return finish()

    qTh = ar.get([128, 2, TOK], BF16)
    kTh = ar.get([128, 2, TOK], BF16)
    vh = ar.get([128, NT, 512], BF16)
    og = ar.get([128, NT, 512], BF16)
    hA = ar.get([128, NT, 512], BF16)
    Cst = ar.get([128, 2, 512], F32)
    Csh = ar.get([128, 2, 512], BF16)
    nn = ar.get([128, 2], F32)
    nsh = ar.get([128, 2], BF16)
    ntmp = ar.get([128, 2], F32)
    hs = ar.get([128, 512], F32)
    hn = ar.get([128, 512], F32)
    ctmp = ar.get([128, 2, 512], F32)
    kp = [ar.get([128, 256], BF16) for _ in range(2)]
    scT = [ar.get([128, 128], BF16) for _ in range(2)]
    yb = ar.get([128, 512], BF16)
    yT = ar.get([128, 4, 128], BF16)
    bvo = ar.get([1, 1024], BF16)
    dn = ar.get([128, 1], F32)
    rr = ar.get([128, 1], F32)
    st6 = ar.get([128, 6], F32)
    mv2 = ar.get([128, 2], F32)
    rstd2 = ar.get([128, 1], F32)
    nb2 = ar.get([128, 1], F32)
    pst5 = PS[5][:].bitcast(BF16)
    kb.op('pool', lambda e: e.memset(hs[:], 0.0), writes=["hs"])
    for hd_ in range(4):
        kb.dma('sp', st_in[hd_].ap()[256:264, :], hs[0:8, :], reads=["hs"], writes=["st_in%dn" % hd_])
    ui = [0]

    def run_pass(Dn, tiles, hd, wo_t, wo_n):
        mask = MLEF if Dn == 0 else MGEF
        for idx, t in enumerate(tiles):
            nxt = tiles[idx + 1] if idx + 1 < len(tiles) else None
            ucol = uu[:, t, Dn, hd:hd + 1]
            flcol = fl[:, t, Dn, hd:hd + 1]
            alcol = al[:, Dn, hd, t:t + 1]
            i2 = ui[0] % 2
            ui[0] += 1
            kpt, kpn = kp[i2], "kp%d" % i2
            sct, scn = scT[i2], "scT%d" % i2
            tsl = slice(t * 128, (t + 1) * 128)
            for dc in range(2):
                kb.op('pe', lambda e: e.transpose(out=pst5[:, dc * 128:(dc + 1) * 128], in_=kTh[:, dc, tsl], identity=IDB[:]),
                      reads=["kTh", "IDB"], writes=["ps5"], inc=(dc == 1))
            for dc in range(2):
                kb.op('pe', lambda e: e.matmul(PS[0][:, 0:128], lhsT=kTh[:, dc, tsl], rhs=qTh[:, dc, tsl], start=(dc == 0), stop=(dc == 1)),
                      reads=["kTh", "qTh"], writes=["ps0"], inc=(dc == 1))
            kb.op('act', lambda e: e.activation(out=kpt[:], in_=pst5[:, 0:256], func=AF.Copy, scale=ucol),
                  reads=["ps5", "uu"], writes=[kpn])
            kb.op('dve', lambda e: e.scalar_tensor_tensor(out=sct[:], in0=PS[0][:, 0:128], scalar=ucol, in1=mask, op0=ALU.mult, op1=ALU.mult),
                  reads=["ps0", "uu", "CST"], writes=[scn])
            for dc in range(2):
                kb.op('pe', lambda e: e.matmul(PS[3 + dc][:], lhsT=kpt[:, dc * 128:(dc + 1) * 128], rhs=vh[:, t, :], start=True, stop=True),
                      reads=[kpn, "vh"], writes=[psname(3 + dc)])
                kb.op('pe', lambda e: e.matmul(PS[2][:, 8 + dc:9 + dc], lhsT=kpt[:, dc * 128:(dc + 1) * 128], rhs=ONESB[:, 0:1], start=True, stop=True),
                      reads=[kpn, "ONESB"], writes=["ps2"])
            for dc in range(2):
                kb.op('pe', lambda e: e.matmul(PS[1][:], lhsT=qTh[:, dc, tsl], rhs=Csh[:, dc, :], start=(dc == 0), stop=False),
                      reads=["qTh", "Csh"], writes=["ps1"], inc=False)
            kb.op('pe', lambda e: e.matmul(PS[1][:], lhsT=sct[:], rhs=vh[:, t, :], start=False, stop=True),
                  reads=[scn, "vh"], writes=["ps1"])
            for dc in range(2):
                kb.op('pe', lambda e: e.matmul(PS[2][:, 0:1], lhsT=qTh[:, dc, tsl], rhs=nsh[:, dc:dc + 1], start=(dc == 0), stop=False),
                      reads=["qTh", "nsh"], writes=["ps2"], inc=False)
            kb.op('pe', lambda e: e.matmul(PS[2][:, 0:1], lhsT=sct[:], rhs=ONESB[:, 0:1], start=False, stop=True),
                  reads=[scn, "ONESB"], writes=["ps2"])
            kb.op('dve', lambda e: e.tensor_scalar(out=rr[:], in0=PS[2][:, 0:1], scalar1=-1.0, scalar2=flcol, op0=ALU.mult, op1=ALU.max),
                  reads=["ps2", "fl"], writes=["rr"])
            kb.op('dve', lambda e: e.tensor_scalar(out=dn[:], in0=PS[2][:, 0:1], scalar1=rr[:], scalar2=None, op0=ALU.max),
                  reads=["ps2", "rr"], writes=["dn"])
            kb.op('dve', lambda e: e.reciprocal(out=rr[:], in_=dn[:]), reads=["dn"], writes=["rr"])
            if Dn == 0:
                kb.op('act', lambda e: e.activation(out=hA[:, t, :], in_=PS[1][:], func=AF.Copy, scale=rr[:]),
                      reads=["ps1", "rr"], writes=["hA"])
            else:
                kb.op('dve', lambda e: e.scalar_tensor_tensor(out=hs[:], in0=PS[1][:], scalar=rr[:], in1=hA[:, t, :], op0=ALU.mult, op1=ALU.add),
                      reads=["ps1", "rr", "hA"], writes=["hs"])
            if nxt is not None:
                for dc in range(2):
                    kb.op('dve', lambda e: e.scalar_tensor_tensor(out=Cst[:, dc, :], in0=Cst[:, dc, :], scalar=alcol, in1=PS[3 + dc][:],
                                                                  op0=ALU.mult, op1=ALU.add),
                          reads=["Cst", "al", psname(3 + dc)], writes=["Cst"])
                kb.op('dve', lambda e: e.scalar_tensor_tensor(out=nn[:], in0=nn[:], scalar=alcol, in1=PS[2][:, 8:10], op0=ALU.mult, op1=ALU.add),
                      reads=["nn", "al", "ps2"], writes=["nn"])
                alnx = al[:, Dn, hd, nxt:nxt + 1]
                kb.op('act', lambda e: e.activation(out=Csh[:], in_=Cst[:], func=AF.Copy, scale=alnx), reads=["Cst", "al"], writes=["Csh"])
                kb.op('act', lambda e: e.activation(out=nsh[:], in_=nn[:], func=AF.Copy, scale=alnx), reads=["nn", "al"], writes=["nsh"])
            elif Dn == 0:
                for dc in range(2):
                    kb.op('dve', lambda e: e.scalar_tensor_tensor(out=Cst[:, dc, :], in0=Cst[:, dc, :], scalar=alcol, in1=PS[3 + dc][:],
                                                                  op0=ALU.mult, op1=ALU.add),
                          reads=["Cst", "al", psname(3 + dc)], writes=["Cst"])
                kb.op('dve', lambda e: e.scalar_tensor_tensor(out=nn[:], in0=nn[:], scalar=alcol, in1=PS[2][:, 8:10], op0=ALU.mult, op1=ALU.add),
                      reads=["nn", "al", "ps2"], writes=["nn"])
            if Dn == 1:
                kb.op('dve', lambda e: e.bn_stats(out=st6[:], in_=hs[:]), reads=["hs"], writes=["st6"])
                kb.op('dve', lambda e: e.bn_aggr(out=mv2[:], in_=st6[:]), reads=["st6"], writes=["mv2"])
                kb.op('dve', lambda e: e.tensor_scalar(out=rstd2[:], in0=mv2[:, 1:2], scalar1=HN_EPS, scalar2=None, op0=ALU.add),
                      reads=["mv2"], writes=["rstd2"])
                kb.op('act', lambda e: e.activation(out=rstd2[:], in_=rstd2[:], func=AF.Sqrt), reads=["rstd2"], writes=["rstd2"])
                kb.op('dve', lambda e: e.reciprocal(out=rstd2[:], in_=rstd2[:]), reads=["rstd2"], writes=["rstd2"])
                kb.op('dve', lambda e: e.scalar_tensor_tensor(out=nb2[:], in0=mv2[:, 0:1], scalar=-1.0, in1=rstd2[:], op0=ALU.mult, op1=ALU.mult),
                      reads=["mv2", "rstd2"], writes=["nb2"])
                kb.op('act', lambda e: e.activation(out=hn[:], in_=hs[:], func=AF.Identity, bias=nb2[:], scale=rstd2[:]),
                      reads=["hs", "nb2", "rstd2"], writes=["hn"])
                kb.op('dve', lambda e: e.tensor_tensor(out=yb[:], in0=hn[:], in1=og[:, t, :], op=ALU.mult), reads=["hn", "og"], writes=["yb"])
                for c in range(4):
                    kb.op('pe', lambda e: e.transpose(out=pst5[:, 256 + c * 128:256 + (c + 1) * 128], in_=yb[:, c * 128:(c + 1) * 128], identity=IDB[:]),
                          reads=["yb", "IDB"], writes=["ps5"], inc=(c == 3))
                kb.op('act', lambda e: e.activation(out=yT[:], in_=pst5[:, 256:768].rearrange("p (c n) -> p c n", c=4), func=AF.Copy),
                      reads=["ps5"], writes=["yT"])
                for cc in range(4):
                    def mm(bank):
                        for c in range(4):
                            kb.op('pe', lambda e: e.matmul(PS[bank][:], lhsT=yT[:, c, :], rhs=wo_t[:, c, cc * 512:(cc + 1) * 512],
                                                           start=(c == 0), stop=(c == 3)),
                                  reads=["yT", wo_n], writes=[psname(bank)], inc=(c == 3))
                    accum_block(t, cc, mm, banks=(6, 7))

    for hd in range(4):
        cb = hd * 1536
        kb.dma('pool', bvo[:], mbvo_in[0:1, hd * 1024:(hd + 1) * 1024], writes=["bvo"])
        bc_from_dram(hn[:], mnw_in[0:1, hd * 512:(hd + 1) * 512], "hn")
        wt, wn = slots.load(kview(mwin_in[:, cb:cb + 512]), (16, 512))
        pi = 0
        for dc in range(2):
            for th in range(2):
                bank = pi % 2
                pi += 1
                for k in range(16):
                    kb.op('pe', lambda e: e.matmul(PS[bank][:], lhsT=wt[:, k, dc * 128:(dc + 1) * 128], rhs=HT[:, k, th * 512:(th + 1) * 512],
                                                   start=(k == 0), stop=(k == 15)),
                          reads=HT_ALL + [wn], writes=[psname(bank)], inc=(k == 15))
                kb.op('dve', lambda e: e.tensor_scalar(out=qTh[:, dc, th * 512:(th + 1) * 512], in0=PS[bank][:],
                                                       scalar1=bmqk[:, hd * 4 + dc:hd * 4 + dc + 1], scalar2=0.0625, op0=ALU.add, op1=ALU.mult),
                      reads=[psname(bank), "bmqk"], writes=["qTh"])
        for dc in range(2):
            for th in range(2):
                bank = pi % 2
                pi += 1
                for k in range(16):
                    kb.op('pe', lambda e: e.matmul(PS[bank][:], lhsT=wt[:, k, 256 + dc * 128:256 + (dc + 1) * 128], rhs=HT[:, k, th * 512:(th + 1) * 512],
                                                   start=(k == 0), stop=(k == 15)),
                          reads=HT_ALL + [wn], writes=[psname(bank)], inc=(k == 15))
                kb.op('act', lambda e: e.activation(out=kTh[:, dc, th * 512:(th + 1) * 512], in_=PS[bank][:], func=AF.Identity,
                                                    bias=bmqk[:, hd * 4 + 2 + dc:hd * 4 + 3 + dc]),
                      reads=[psname(bank), "bmqk"], writes=["kTh"])
        wt, wn = slots.load(kview(mwin_in[:, cb + 512:cb + 1024]), (16, 512))
        for t in range(NT):
            bank = 2 + (t % 2)
            for k in range(16):
                kb.op('pe', lambda e: e.matmul(PS[bank][:], lhsT=HT[:, k, t * 128:(t + 1) * 128], rhs=wt[:, k, :], start=(k == 0), stop=False),
                      reads=HT_ALL + [wn], writes=[psname(bank)], inc=False)
            kb.op('pe', lambda e: e.matmul(PS[bank][:], lhsT=ONESB[0:1, :], rhs=bvo[0:1, 0:512], start=False, stop=True),
                  reads=["ONESB", "bvo"], writes=[psname(bank)])
            kb.op('act', lambda e: e.activation(out=vh[:, t, :], in_=PS[bank][:], func=AF.Copy), reads=[psname(bank)], writes=["vh"])
        wt, wn = slots.load(kview(mwin_in[:, cb + 1024:cb + 1536]), (16, 512))
        for t in range(NT):
            bank = 2 + (t % 2)
            for k in range(16):
                kb.op('pe', lambda e: e.matmul(PS[bank][:], lhsT=HT[:, k, t * 128:(t + 1) * 128], rhs=wt[:, k, :], start=(k == 0), stop=False),
                      reads=HT_ALL + [wn], writes=[psname(bank)], inc=False)
            kb.op('pe', lambda e: e.matmul(PS[bank][:], lhsT=ONESB[0:1, :], rhs=bvo[0:1, 512:1024], start=False, stop=True),
                  reads=["ONESB", "bvo"], writes=[psname(bank)])
            kb.op('act', lambda e: e.activation(out=hs[:], in_=PS[bank][:], func=AF.Sigmoid), reads=[psname(bank)], writes=["hs"])
            kb.op('dve', lambda e: e.tensor_tensor(out=og[:, t, :], in0=hs[:], in1=hn[:], op=ALU.mult), reads=["hs", "hn"], writes=["og"])
        kb.op('pool', lambda e: e.memset(Cst[:], 0.0), writes=["Cst"])
        kb.op('pool', lambda e: e.memset(nn[:], 0.0), writes=["nn"])
        kb.op('pool', lambda e: e.memset(Csh[:], 0.0), writes=["Csh"])
        kb.op('pool', lambda e: e.memset(nsh[:], 0.0), writes=["nsh"])
        run_pass(0, list(range(NT)), hd, None, None)
        stn = "st_in%d" % hd
        son = "st_out%d" % hd
        kb.dma('sp', st_in[hd].ap()[0:256, :].rearrange("(c p) n -> p c n", p=128), Cst[:], reads=["Cst"], writes=[stn])
        with nc.allow_non_contiguous_dma(reason="tiny n vector"):
            kb.dma('sp', st_in[hd].ap()[256:258, 0:128].rearrange("c p -> p c"), nn[:], reads=["nn"], writes=[stn + "n"])
        collective(PAIRS, st_in[hd], st_out[hd], [stn, stn + "n"], [son])
        for s_ in range(2):
            kb.dma('sp', ctmp[:], st_out[hd].ap()[s_ * 264:s_ * 264 + 256, :].rearrange("(c p) n -> p c n", p=128), reads=[son], writes=["ctmp"])
            with nc.allow_non_contiguous_dma(reason="tiny n vector"):
                kb.dma('sp', ntmp[:], st_out[hd].ap()[s_ * 264 + 256:s_ * 264 + 258, 0:128].rearrange("c p -> p c"), reads=[son], writes=["ntmp"])
            if s_ == 0:
                kb.op('dve', lambda e: e.tensor_scalar(out=Cst[:], in0=ctmp[:], scalar1=SEL[:, 0:1], scalar2=None, op0=ALU.mult),
                      reads=["ctmp", "SEL"], writes=["Cst"])
                kb.op('dve', lambda e: e.tensor_scalar(out=nn[:], in0=ntmp[:], scalar1=SEL[:, 0:1], scalar2=None, op0=ALU.mult),
                      reads=["ntmp", "SEL"], writes=["nn"])
            else:
                kb.op('dve', lambda e: e.scalar_tensor_tensor(out=Cst[:], in0=ctmp[:], scalar=SEL[:, 1:2], in1=Cst[:], op0=ALU.mult, op1=ALU.add),
                      reads=["ctmp", "SEL", "Cst"], writes=["Cst"])
                kb.op('dve', lambda e: e.scalar_tensor_tensor(out=nn[:], in0=ntmp[:], scalar=SEL[:, 1:2], in1=nn[:], op0=ALU.mult, op1=ALU.add),
                      reads=["ntmp", "SEL", "nn"], writes=["nn"])
        al0 = al[:, 1, hd, NT - 1:NT]
        kb.op('act', lambda e: e.activation(out=Csh[:], in_=Cst[:], func=AF.Copy, scale=al0), reads=["Cst", "al"], writes=["Csh"])
        kb.op('act', lambda e: e.activation(out=nsh[:], in_=nn[:], func=AF.Copy, scale=al0), reads=["nn", "al"], writes=["nsh"])
        wo_t, wo_n = slots.load(mwo_in[hd * 512:(hd + 1) * 512, :].rearrange("(c p) n -> p c n", p=128), (4, D))
        run_pass(1, list(range(NT - 1, -1, -1)), hd, wo_t, wo_n)

    layer_norm(2)
    if stop_after == "mlstm":
        return finish()
    mlp(1, 3)
    return finish()


def _consts():
    cst = np.zeros((128, 1024), np.float32)
    p = np.arange(128)
    cst[:, 0:128] = np.eye(128, dtype=np.float32)
    cst[:, 128:256] = (p[:, None] >= p[None, :]).astype(np.float32)
    cst[:, 256:384] = (p[:, None] <= p[None, :]).astype(np.float32)
    inv_freq = 1.0 / (10000.0 ** (np.arange(0, 64, 2, dtype=np.float32) / 64.0))
    cst[:, 384] = inv_freq[p % 32]
    cst[:, 385] = np.where(p >= 64, -1.0, 1.0)
    km0 = ((p // 32) % 2 == 0).astype(np.float32)
    cst[:, 386] = km0
    cst[:, 387] = 1.0 - km0
    cst[0:4, 388:392] = np.eye(4, dtype=np.float32)
    return cst


def _qk_perm():
    cols = []
    for c in range(16):
        m, g = c // 4, c % 4
        ha, hb = (2 * m) * 4 + g, (2 * m + 1) * 4 + g
        cols += [ha * 64 + d for d in range(32)] + [hb * 64 + d for d in range(32)]
        cols += [ha * 64 + 32 + d for d in range(32)] + [hb * 64 + 32 + d for d in range(32)]
    for m in range(4):
        ha, hb = 2 * m, 2 * m + 1
        base = 2048
        cols += [base + ha * 64 + d for d in range(32)] + [base + hb * 64 + d for d in range(32)]
        cols += [base + ha * 64 + 32 + d for d in range(32)] + [base + hb * 64 + 32 + d for d in range(32)]
    return np.array(cols)


def _wo_perm():
    rows = []
    for c in range(16):
        m, g = c // 4, c % 4
        for e in range(2):
            hq = (2 * m + e) * 4 + g
            rows += [hq * 64 + d for d in range(64)]
    return np.array(rows)


def _mlstm_cols(hf):
    cols = []
    for hd in range(4):
        cols += list(range(hd * 256, (hd + 1) * 256))
        cols += list(range(1024 + hd * 256, 1024 + (hd + 1) * 256))
        cols += list(range(2048 + hd * 512, 2048 + (hd + 1) * 512))
        cols += list(range(4096 + hd * 512, 4096 + (hd + 1) * 512))
    g0 = 6144
    if hf == 0:
        gcols = [g0 + i for i in range(16)]
    else:
        gcols = [g0 + 8 + i for i in range(8)] + [g0 + i for i in range(8)]
    return np.array(cols), np.array(gcols)


def prep_inputs(inputs):
    f = lambda a: np.ascontiguousarray(np.asarray(a))
    x = f(inputs["x"]); c = f(inputs["c"]); pos = f(inputs["positions"])
    cst = _consts()
    qkp = _qk_perm()
    wop = _wo_perm()
    wqkv = f(inputs["attn_w_qkv"])[0]
    bqkv = f(inputs["attn_b_qkv"])[0]
    parts = []
    for m in range(4):
        parts.append(wqkv[:, qkp[m * 512:(m + 1) * 512]])
        parts.append(wqkv[:, qkp[2048 + m * 128:2048 + (m + 1) * 128]])
        parts.append(wqkv[:, 2560 + m * 128:2560 + (m + 1) * 128])
    wqkv_p = np.ascontiguousarray(np.concatenate(parts, axis=1))
    bqk = np.ascontiguousarray(bqkv[qkp].reshape(20, 128).T)
    bv = np.ascontiguousarray(bqkv[2560:3072][None, :])
    sink = f(inputs["attn_sink"])[0][None, :]
    awo = np.ascontiguousarray(f(inputs["attn_w_o"])[0][wop, :])
    abo = f(inputs["attn_b_o"])[0][None, :]
    mw = f(inputs["mlstm_w_in"])[0]
    mb = f(inputs["mlstm_b_in"])[0]
    mnw = f(inputs["mlstm_norm_w"])[0][None, :]
    mwo = f(inputs["mlstm_w_o"])[0]
    mbo = f(inputs["mlstm_b_o"])[0][None, :]
    mod_w = f(inputs["mod_w"]); mod_b = f(inputs["mod_b"])
    w1 = f(inputs["mlp_w1"]); b1 = f(inputs["mlp_b1"]); w2 = f(inputs["mlp_w2"]); b2 = f(inputs["mlp_b2"])
    b1c = np.ascontiguousarray(b1.reshape(2, 64, 128).transpose(0, 2, 1))
    b2r = np.ascontiguousarray(b2[:, None, :])
    lng = np.ascontiguousarray(np.stack([inputs["ln_mix_g"][0], inputs["ln_mlp_g"][0], inputs["ln_mix_g"][1], inputs["ln_mlp_g"][1]])[:, None, :]).astype(np.float32)
    lnb = np.ascontiguousarray(np.stack([inputs["ln_mix_b"][0], inputs["ln_mlp_b"][0], inputs["ln_mix_b"][1], inputs["ln_mlp_b"][1]])[:, None, :]).astype(np.float32)
    cT = np.ascontiguousarray(c.reshape(4, 16, 128).transpose(2, 1, 0))
    mlstm_variants = {}
    for hf in range(2):
        cols, gcols = _mlstm_cols(hf)
        mwin = np.ascontiguousarray(mw[:, cols])
        mwg = np.ascontiguousarray(mw[:, gcols])
        bq = mb[cols].reshape(4, 1536)
        mbqk = np.ascontiguousarray(bq[:, 0:512].reshape(4, 4, 128).transpose(2, 0, 1).reshape(128, 16))
        mbvo = np.ascontiguousarray(bq[:, 512:1536].reshape(1, 4096))
        mbg = np.ascontiguousarray(mb[gcols][None, :])
        mlstm_variants[hf] = (mwin, mwg, mbqk, mbvo, mbg)
    in_maps = []
    for r in range(8):
        b, hf = r // 2, r % 2
        idx = np.arange(TOKH) if hf == 0 else (2047 - np.arange(TOKH))
        x_loc = np.ascontiguousarray(x[b, idx, :])
        pos_loc = np.ascontiguousarray(pos[b, idx][None, :]).astype(np.int32)
        modw = np.ascontiguousarray(mod_w.reshape(2, D, 6, 8, 256)[:, :, :, r, :].transpose(1, 0, 2, 3).reshape(D, 3072))
        modb = np.ascontiguousarray(mod_b.reshape(2, 6, 8, 256)[:, :, r, :].reshape(1, 3072))
        onehot = np.zeros((4, 128), np.float32); onehot[b, :] = 1.0
        sel = np.zeros((128, 2), np.float32); sel[:, 1 - hf] = 1.0
        mwin, mwg, mbqk, mbvo, mbg = mlstm_variants[hf]
        in_maps.append({
            "x_loc": x_loc, "pos_loc": pos_loc, "cT": cT, "modw": modw, "modb": modb, "onehot": onehot, "sel": sel,
            "cst": cst, "wqkv": wqkv_p, "bqk": bqk, "bv": bv, "sink": sink, "awo": awo, "abo": abo,
            "mwin": mwin, "mwg": mwg, "mbqk": mbqk, "mbvo": mbvo, "mbg": mbg, "mnw": mnw, "mwo": mwo, "mbo": mbo,
            "w1": w1, "b1c": b1c, "w2": w2, "b2": b2r, "lng": lng, "lnb": lnb,
        })
    return in_maps


def assemble(results):
    out = np.zeros((4, 2048, D), np.float32)
    for r in range(8):
        b, hf = r // 2, r % 2
        y = np.asarray(results[r]["y_out"])
        if hf == 0:
            out[b, 0:1024] = y
        else:
            out[b, 1024:2048] = y[::-1]
    return out


_NC_CACHE = {}


def kernel(**inputs):
    in_maps = prep_inputs(inputs)
    if "nc" not in _NC_CACHE:
        _NC_CACHE["nc"] = build()
    res = run_bass_kernel_spmd(_NC_CACHE["nc"], in_maps, core_ids=list(range(8)))
    return assemble(res.results)
```

```python
import numpy as np
import ml_dtypes
import concourse.bass as bass
import concourse.mybir as mybir
from concourse.bass_utils import run_bass_kernel_spmd

F32 = mybir.dt.float32
BF16 = mybir.dt.bfloat16
I32 = mybir.dt.int32
AF = mybir.ActivationFunctionType
ALU = mybir.AluOpType
AX = mybir.AxisListType

D = 2048
NT = 8
TOK = 1024
TOKH = 1152
ALPHA = 4.0 ** 0.25
LN_EPS = 1e-5
HN_EPS = 1e-6
PI = float(np.pi)


class KB:
    def __init__(self, nc):
        self.nc = nc
        self.eng = {'pe': nc.tensor, 'act': nc.scalar, 'dve': nc.vector, 'pool': nc.gpsimd, 'sp': nc.sync}
        self.sem = {e: nc.alloc_semaphore("s_" + e) for e in ['pe', 'act', 'dve', 'pool']}
        self.cnt = {e: 0 for e in self.sem}
        self.waited = {}
        self.swdge_serial = True
        self.res = {}
        self.dsem = {}
        self.dcnt = {}
        self.rr = {'sp': 0, 'pool': 0, 'act': 0}
        self.nrr = {'sp': 16, 'pool': 8, 'act': 2}
        for q, n in self.nrr.items():
            for i in range(n):
                self._mk_dsem("rr%s%d" % (q, i))

    def _mk_dsem(self, name):
        self.dsem[name] = self.nc.alloc_semaphore("d_" + name)
        self.dcnt[name] = 0

    def _semof(self, key):
        return self.sem[key] if key in self.sem else self.dsem[key]

    def _deps(self, reads, writes):
        deps = []
        for r in reads:
            st = self.res.get(r)
            if st and st['w']:
                deps.append(st['w'])
        for w in writes:
            st = self.res.get(w)
            if st:
                if st['w']:
                    deps.append(st['w'])
                deps.extend(st['r'])
        return deps

    def _wait(self, engine, deps):
        best = {}
        for (key, c) in deps:
            if key == engine and engine == 'pe':
                continue
            if best.get(key, 0) < c:
                best[key] = c
        for key, c in best.items():
            if self.waited.get((engine, key), 0) >= c:
                continue
            if key in self.cnt:
                assert c <= self.cnt[key], (engine, key, c, self.cnt[key])
            self.eng[engine].wait_ge(self._semof(key), c)
            self.waited[(engine, key)] = c

    def _mark(self, ev, reads, writes):
        for r in reads:
            st = self.res.setdefault(r, {'w': None, 'r': []})
            st['r'].append(ev)
            if len(st['r']) > 24:
                best = {}
                for (k, c) in st['r']:
                    best[k] = max(best.get(k, 0), c)
                st['r'] = list(best.items())
        for w in writes:
            self.res[w] = {'w': ev, 'r': []}

    def op(self, engine, fn, reads=(), writes=(), inc=True):
        self._wait(engine, self._deps(reads, writes))
        ins = fn(self.eng[engine])
        if inc:
            ins.then_inc(self.sem[engine], 1)
            self.cnt[engine] += 1
            ev = (engine, self.cnt[engine])
        else:
            assert engine == 'pe'
            ev = (engine, self.cnt[engine] + 1)
        self._mark(ev, reads, writes)
        return ins

    def dma(self, q, out, in_, reads=(), writes=(), sem=None, **kw):
        if sem is None:
            sem = "rr%s%d" % (q, self.rr[q])
            self.rr[q] = (self.rr[q] + 1) % self.nrr[q]
        if sem not in self.dsem:
            self._mk_dsem(sem)
        if q == 'pool' and self.swdge_serial:
            reads = list(reads) + ["_swdge"]
            writes = list(writes) + ["_swdge"]
        self._wait(q, self._deps(reads, writes))
        ins = self.eng[q].dma_start(out=out, in_=in_, **kw)
        ins.then_inc(self.dsem[sem], 16)
        self.dcnt[sem] += 16
        ev = (sem, self.dcnt[sem])
        self._mark(ev, reads, writes)
        return ev

    def barrier(self):
        for e in ['pe', 'act', 'dve', 'pool', 'sp']:
            for key in self.cnt:
                if key == e and e == 'pe':
                    continue
                c = self.cnt[key]
                if c > 0 and self.waited.get((e, key), 0) < c:
                    self.eng[e].wait_ge(self.sem[key], c)
                    self.waited[(e, key)] = c
            for key, c in self.dcnt.items():
                if c > 0 and self.waited.get((e, key), 0) < c:
                    self.eng[e].wait_ge(self.dsem[key], c)
                    self.waited[(e, key)] = c

    def wait_all_on(self, engine, names):
        deps = []
        for n in names:
            st = self.res.get(n)
            if st:
                if st['w']:
                    deps.append(st['w'])
                deps.extend(st['r'])
        self._wait(engine, deps)


def build(stop_after=None):
    nc = bass.Bass("TRN2", target_bir_lowering=False)
    kb = KB(nc)

    def din(name, shape, dt=F32):
        return nc.dram_tensor(name, list(shape), dt, kind="ExternalInput").ap()

    x_in = din("x_loc", [TOKH, D])
    pos_in = din("pos_loc", [1, TOKH], I32)
    cT_in = din("cT", [128, 16, 4])
    modw_in = din("modw", [D, 3072])
    modb_in = din("modb", [1, 3072])
    onehot_in = din("onehot", [4, 128])
    sel_in = din("sel", [128, 2])
    cst_in = din("cst", [128, 1024])
    wqkv_in = din("wqkv", [D, 3072])
    bqk_in = din("bqk", [128, 20])
    bv_in = din("bv", [1, 512])
    sink_in = din("sink", [1, 32])
    awo_in = din("awo", [D, D])
    abo_in = din("abo", [1, D])
    mwin_in = din("mwin", [D, 6144])
    mwg_in = din("mwg", [D, 16])
    mbqk_in = din("mbqk", [128, 16])
    mbvo_in = din("mbvo", [1, 4096])
    mbg_in = din("mbg", [1, 16])
    mnw_in = din("mnw", [1, D])
    mwo_in = din("mwo", [D, D])
    mbo_in = din("mbo", [1, D])
    w1_in = din("w1", [2, D, 4 * D])
    b1_in = din("b1c", [2, 128, 64])
    w2_in = din("w2", [2, 4 * D, D])
    b2_in = din("b2", [2, 1, D])
    lng_in = din("lng", [4, 1, D])
    lnb_in = din("lnb", [4, 1, D])
    y_out = nc.dram_tensor("y_out", [TOK, D], F32, kind="ExternalOutput").ap()

    ag_in = nc.dram_tensor("ag_in", [4, 3072], F32)
    ag_out = nc.dram_tensor("ag_out", [32, 3072], F32)
    mx_in = nc.dram_tensor("mx_in", [1, 8], F32)
    mx_out = nc.dram_tensor("mx_out", [2, 8], F32)
    st_in = [nc.dram_tensor("st_in%d" % h, [264, 512], F32) for h in range(4)]
    st_out = [nc.dram_tensor("st_out%d" % h, [528, 512], F32) for h in range(4)]

    def sb(name, shape, dt):
        return nc.alloc_sbuf_tensor(name, list(shape), dt)

    X = sb("X", [128, NT, D], F32)
    HT = sb("HT", [128, 16, TOKH], BF16)
    CST = sb("CST", [128, 1024], F32)
    IDB = sb("IDB", [128, 128], BF16)
    MGE = sb("MGE", [128, 512], BF16)
    MLE = sb("MLE", [128, 512], BF16)
    ONESB = sb("ONESB", [128, 128], BF16)
    ONESF = sb("ONESF", [128, 128], F32)
    SEL = sb("SEL", [128, 2], F32)
    OH = sb("OH", [4, 128], F32)
    LNEPS = sb("LNEPS", [128, 1], F32)
    WORK = sb("WORK", [128, 100 * 1024], mybir.dt.uint8)

    IDF = CST[:, 0:128]
    MGEF = CST[:, 128:256]
    MLEF = CST[:, 256:384]
    INVF = CST[:, 384:385]
    SGN = CST[:, 385:386]
    KM0 = CST[:, 386:387]
    KM1 = CST[:, 387:388]
    EYE4 = CST[0:4, 388:392]

    class Arena:
        def __init__(self):
            self.off = 0

        def reset(self, off=0):
            self.off = off

        def get(self, shape, dt):
            nbytes = int(np.prod(shape[1:])) * mybir.dt.size(dt)
            nbytes_al = (nbytes + 31) // 32 * 32
            assert self.off + nbytes_al <= 100 * 1024, (self.off, nbytes_al)
            v = WORK[0:shape[0], self.off:self.off + nbytes].bitcast(dt)
            self.off += nbytes_al
            if len(shape) > 2:
                names = " ".join("d%d" % i for i in range(1, len(shape)))
                kw = {"d%d" % i: shape[i] for i in range(1, len(shape))}
                v = v.rearrange("p (%s) -> p %s" % (names, names), **kw)
            return v

    ar = Arena()

    PS = [nc.alloc_psum_tensor("ps%d" % i, [128, 512], F32) for i in range(8)]

    def psname(i):
        return "ps%d" % i

    kb.dma('sp', CST[:], cst_in, writes=["CST"])
    kb.dma('sp', SEL[:], sel_in, writes=["SEL"])
    kb.dma('sp', OH[:], onehot_in, writes=["OH"])
    kb.op('dve', lambda e: e.tensor_copy(out=IDB[:], in_=IDF), reads=["CST"], writes=["IDB"])
    kb.op('dve', lambda e: e.tensor_copy(out=MGE[:].rearrange("p (a b) -> p a b", a=4),
                                         in_=MGEF.unsqueeze(1).to_broadcast([128, 4, 128])), reads=["CST"], writes=["MGE"])
    kb.op('dve', lambda e: e.tensor_copy(out=MLE[:].rearrange("p (a b) -> p a b", a=4),
                                         in_=MLEF.unsqueeze(1).to_broadcast([128, 4, 128])), reads=["CST"], writes=["MLE"])
    kb.op('pool', lambda e: e.memset(ONESB[:], 1.0), writes=["ONESB"])
    kb.op('pool', lambda e: e.memset(ONESF[:], 1.0), writes=["ONESF"])
    kb.op('pool', lambda e: e.memset(LNEPS[:], LN_EPS), writes=["LNEPS"])

    for t in range(NT):
        kb.dma('sp', X[:, t, :], x_in[t * 128:(t + 1) * 128, :], writes=["X%d" % t])


    SLOT_BYTES = 16 * 1024

    class Slots:
        def __init__(self, n):
            self.n = n
            self.raw = [ar.get([128, SLOT_BYTES // 2], BF16) for _ in range(n)]
            self.i = 0

        def load(self, src_ap, shape3):
            s = self.i % self.n
            self.i += 1
            rn = "slot%d" % s
            a, b = shape3
            v = self.raw[s][:, 0:a * b].rearrange("p (a b) -> p a b", a=a)
            kb.dma('pool', v, src_ap, writes=[rn], sem="w_" + rn)
            return v, rn

    def kview(ap2d):
        return ap2d.rearrange("(k p) n -> p k n", p=128)

    def bc_from_dram(dst, src_row, name, q='sp'):
        kb.dma(q, dst, src_row.partition_broadcast(128), writes=[name])

    mb_i = [0]

    def mod_bc(dst, layer, typ, name, add_one, tmp4):
        for cc in range(4):
            i = mb_i[0] % 2
            mb_i[0] += 1
            tn = "tmp4_%d" % i
            c0 = layer * 1536 + typ * 256 + (cc * 512) % 256
            src = ag_out.ap()[8 * cc:8 * cc + 8, layer * 1536 + typ * 256: layer * 1536 + typ * 256 + 256]
            src = src.rearrange("(r b) f -> b r f", b=4)
            kb.dma('sp', tmp4[i][:].rearrange("b (r f) -> b r f", r=2), src, reads=["ag_out"], writes=[tn])
            bank = 4 + cc
            kb.op('pe', lambda e: e.matmul(PS[bank][:], lhsT=OH[:], rhs=tmp4[i][:], start=True, stop=True),
                  reads=["OH", tn], writes=[psname(bank)])
            if add_one:
                kb.op('dve', lambda e: e.tensor_scalar(out=dst[:, cc * 512:(cc + 1) * 512], in0=PS[bank][:],
                                                       scalar1=1.0, scalar2=None, op0=ALU.add),
                      reads=[psname(bank)], writes=[name])
            else:
                kb.op('dve', lambda e: e.tensor_copy(out=dst[:, cc * 512:(cc + 1) * 512], in_=PS[bank][:]),
                      reads=[psname(bank)], writes=[name])

    HT_ALL = ["HT_%d" % t for t in range(9)]

    def modulate_prep(layer, tsc, tsh, tgate, bias_row_ap, ntile):
        kb.barrier()
        ar.reset(P0)
        sc1 = ar.get([128, D], F32)
        sh = ar.get([128, D], F32)
        tb = ar.get([128, D], F32)
        tmp4 = [ar.get([4, 512], F32) for _ in range(2)]
        tmpf = [ar.get([128, D], F32) for _ in range(2)]
        hb = [ar.get([128, D], BF16) for _ in range(2)]
        if ntile == 9:
            kb.dma('sp', tmpf[1][:], x_in[1024:1152, :], writes=["mt_tmpf1"])
        bc_from_dram(tb[:], bias_row_ap, "tb")
        mod_bc(sc1, layer, tsc, "sc1", True, tmp4)
        mod_bc(sh, layer, tsh, "sh", False, tmp4)
        mod_bc(GATE, layer, tgate, "gate", True, tmp4)
        kb.op('pool', lambda e: e.tensor_copy(out=GATEB[:], in_=GATE[:]), reads=["gate"], writes=["gateb"])
        kb.op('pool', lambda e: e.tensor_tensor(out=tb[:], in0=tb[:], in1=GATE[:], op=ALU.mult), reads=["tb", "gate"], writes=["tb"])
        order = ([8] if ntile == 9 else []) + list(range(NT))
        for idx, t in enumerate(order):
            i2 = idx % 2
            if t == 8:
                i2 = 0
                src, srcn = tmpf[1][:], "mt_tmpf1"
            else:
                src, srcn = X[:, t, :], "X%d" % t
            tfn = "mt_tmpf%d" % i2
            hbt, hbn = hb[i2], "mt_hb%d" % i2
            kb.op('dve', lambda e: e.tensor_tensor(out=tmpf[i2][:], in0=src, in1=sc1[:], op=ALU.mult),
                  reads=[srcn, "sc1"], writes=[tfn])
            kb.op('pool', lambda e: e.tensor_tensor(out=hbt[:], in0=tmpf[i2][:], in1=sh[:], op=ALU.add),
                  reads=[tfn, "sh"], writes=[hbn])
            if t < NT:
                kb.op('dve', lambda e: e.scalar_tensor_tensor(out=X[:, t, :], in0=X[:, t, :], scalar=ALPHA, in1=tb[:],
                                                              op0=ALU.mult, op1=ALU.add),
                      reads=["X%d" % t, "tb"], writes=["X%d" % t])
            for half in range(2):
                bank = (2 * idx + half) % 4
                pst = PS[bank][:].bitcast(BF16)
                for j in range(8):
                    c = half * 8 + j
                    kb.op('pe', lambda e: e.transpose(out=pst[:, j * 128:(j + 1) * 128], in_=hbt[:, c * 128:(c + 1) * 128],
                                                      identity=IDB[:]),
                          reads=[hbn, "IDB"], writes=[psname(bank)], inc=(j == 7))
                kb.op('act', lambda e: e.activation(out=HT[:, half * 8:(half + 1) * 8, t * 128:(t + 1) * 128],
                                                    in_=pst.rearrange("p (c n) -> p c n", c=8), func=AF.Copy),
                      reads=[psname(bank)], writes=["HT_%d" % t])
        kb.barrier()
        ar.reset(P0)

    pa_state = {"i": 0}

    def accum_block(tt, cc, emit_mm, banks=(6, 7)):
        bank = banks[pa_state["i"] % len(banks)]
        pa_state["i"] += 1
        emit_mm(bank)
        kb.op('dve', lambda e: e.tensor_tensor(out=X[:, tt, cc * 512:(cc + 1) * 512], in0=PS[bank][:],
                                               in1=X[:, tt, cc * 512:(cc + 1) * 512], op=ALU.add),
              reads=[psname(bank), "X%d" % tt], writes=["X%d" % tt])

    def load_wrows(w_rows_ap):
        return slots.load(w_rows_ap.rearrange("(c p) n -> p c n", p=128), (4, D))

    def gate_fold(wt, wn):
        kb.op('dve', lambda e: e.tensor_tensor(out=wt, in0=wt, in1=GATEB[:].unsqueeze(1).to_broadcast([128, 4, D]), op=ALU.mult),
              reads=[wn, "gateb"], writes=[wn])

    def proj_accum(lhs_view, lhs_names, wt, wn, banks=(4, 5, 6, 7)):
        gate_fold(wt, wn)
        for tt in range(NT):
            for cc in range(4):
                def mm(bank):
                    for c in range(4):
                        kb.op('pe', lambda e: e.matmul(PS[bank][:], lhsT=lhs_view(c)[:, tt * 128:(tt + 1) * 128],
                                                       rhs=wt[:, c, cc * 512:(cc + 1) * 512], start=(c == 0), stop=(c == 3)),
                              reads=lhs_names + [wn], writes=[psname(bank)], inc=(c == 3))
                accum_block(tt, cc, mm, banks)

    def layer_norm(lidx):
        kb.barrier()
        ar.reset(P0)
        lng = ar.get([128, D], F32)
        lnb = ar.get([128, D], F32)
        st = [ar.get([128, 24], F32) for _ in range(2)]
        mv = [ar.get([128, 2], F32) for _ in range(2)]
        rstd = [ar.get([128, 1], F32) for _ in range(2)]
        nb = [ar.get([128, 1], F32) for _ in range(2)]
        bc_from_dram(lng[:], lng_in[lidx], "lng")
        bc_from_dram(lnb[:], lnb_in[lidx], "lnb")

        def s0a(t):
            i = t % 2
            xn = "X%d" % t
            for j in range(4):
                kb.op('dve', lambda e: e.bn_stats(out=st[i][:, j * 6:(j + 1) * 6], in_=X[:, t, j * 512:(j + 1) * 512]),
                      reads=[xn], writes=["ln_st%d" % i])
            kb.op('dve', lambda e: e.bn_aggr(out=mv[i][:], in_=st[i][:]), reads=["ln_st%d" % i], writes=["ln_mv%d" % i])
            kb.op('act', lambda e: e.activation(out=rstd[i][:], in_=mv[i][:, 1:2], func=AF.Sqrt, bias=LNEPS[:]),
                  reads=["ln_mv%d" % i, "LNEPS"], writes=["ln_rstd%d" % i])

        def s0b(t):
            i = t % 2
            kb.op('dve', lambda e: e.reciprocal(out=rstd[i][:], in_=rstd[i][:]), reads=["ln_rstd%d" % i], writes=["ln_rstd%d" % i])
            kb.op('dve', lambda e: e.scalar_tensor_tensor(out=nb[i][:], in0=mv[i][:, 0:1], scalar=-1.0, in1=rstd[i][:], op0=ALU.mult, op1=ALU.mult),
                  reads=["ln_mv%d" % i, "ln_rstd%d" % i], writes=["ln_nb%d" % i])

        def s12(t):
            i = t % 2
            xn = "X%d" % t
            kb.op('act', lambda e: e.activation(out=X[:, t, :], in_=X[:, t, :], func=AF.Identity, bias=nb[i][:], scale=rstd[i][:]),
                  reads=[xn, "ln_nb%d" % i, "ln_rstd%d" % i], writes=[xn])
            kb.op('pool', lambda e: e.tensor_tensor(out=X[:, t, :], in0=X[:, t, :], in1=lng[:], op=ALU.mult),
                  reads=[xn, "lng"], writes=[xn])

        def s3(t):
            xn = "X%d" % t
            kb.op('dve', lambda e: e.tensor_tensor(out=X[:, t, :], in0=X[:, t, :], in1=lnb[:], op=ALU.add),
                  reads=[xn, "lnb"], writes=[xn])

        for i in range(NT + 2):
            if i < NT:
                s0a(i)
            if i >= 2:
                s3(i - 2)
            if i < NT:
                s0b(i)
                s12(i)
        kb.barrier()

    def finish():
        kb.barrier()
        for t in range(NT):
            kb.dma('sp', y_out[t * 128:(t + 1) * 128, :], X[:, t, :], reads=["X%d" % t], writes=["yout%d" % t])
        kb.wait_all_on('sp', ["yout%d" % t for t in range(NT)])
        return nc

    ccsem = nc.alloc_semaphore("ccsem")
    cc_count = [0]
    kb.dsem["cc"] = ccsem
    kb.dcnt["cc"] = 0

    def collective(groups, src, dst, reads, writes):
        kb._wait('pool', kb._deps(reads, writes))
        nc.gpsimd.collective_compute("AllGather", ALU.bypass, replica_groups=groups,
                                     ins=[src.ap().opt()], outs=[dst.ap().opt()]).then_inc(ccsem)
        kb.dcnt["cc"] += 1
        kb._mark(("cc", kb.dcnt["cc"]), reads, writes)

    PAIRS = [[0, 1], [2, 3], [4, 5], [6, 7]]

    ar.reset(0)
    slots = Slots(2)
    S0 = ar.off
    GATE = ar.get([128, D], F32)
    GATEB = ar.get([128, D], BF16)
    P0 = ar.off

    cT = ar.get([128, 16, 4], F32)
    cA = ar.get([128, 16, 4], BF16)
    modsb = ar.get([4, 3072], F32)
    modb_bf = ar.get([1, 3072], BF16)
    kb.dma('sp', cT[:], cT_in, writes=["cT"])
    kb.dma('pool', modb_bf[:], modb_in, writes=["modb_bf"])
    kb.op('act', lambda e: e.activation(out=cA[:], in_=cT[:], func=AF.Silu), reads=["cT"], writes=["cA"])
    for pc in range(6):
        wt, wn = slots.load(kview(modw_in[:, pc * 512:(pc + 1) * 512]), (16, 512))
        bank = pc % 2
        for k in range(16):
            kb.op('pe', lambda e: e.matmul(PS[bank][0:4, :], lhsT=cA[:, k, :], rhs=wt[:, k, :], start=(k == 0), stop=False),
                  reads=["cA", wn], writes=[psname(bank)], inc=False)
        kb.op('pe', lambda e: e.matmul(PS[bank][0:4, :], lhsT=ONESB[0:1, 0:4], rhs=modb_bf[0:1, pc * 512:(pc + 1) * 512],
                                       start=False, stop=True), reads=["ONESB", "modb_bf"], writes=[psname(bank)])
        kb.op('dve', lambda e: e.tensor_copy(out=modsb[:, pc * 512:(pc + 1) * 512], in_=PS[bank][0:4, :]),
              reads=[psname(bank)], writes=["modsb"])
    kb.dma('pool', ag_in.ap(), modsb[:], reads=["modsb"], writes=["ag_in"])
    collective([list(range(8))], ag_in, ag_out, ["ag_in"], ["ag_out"])

    modulate_prep(0, 1, 0, 2, abo_in, 9)

    cosT = ar.get([128, TOKH], F32)
    sinS = ar.get([128, TOKH], F32)
    bqk = ar.get([128, 20], F32)
    bv_bf = ar.get([1, 512], BF16)
    sinkrows = [ar.get([1, 4, 128], BF16) for _ in range(2)]
    sinkf = ar.get([1, 32], F32)
    markA = ar.off
    angt = ar.get([128, TOKH], F32)
    posi = ar.get([128, TOKH], I32)
    kint = ar.get([128, TOKH], I32)
    kb.dma('sp', posi[:], pos_in.partition_broadcast(128), writes=["posi"])
    kb.dma('sp', bqk[:], bqk_in, writes=["bqk"])
    kb.dma('pool', bv_bf[:], bv_in, writes=["bv_bf"])
    kb.dma('sp', sinkf[:], sink_in, writes=["sinkf"])
    kb.op('dve', lambda e: e.tensor_copy(out=angt[:], in_=posi[:]), reads=["posi"], writes=["angt"])
    kb.op('dve', lambda e: e.tensor_scalar(out=angt[:], in0=angt[:], scalar1=INVF, scalar2=None, op0=ALU.mult),
          reads=["angt", "CST"], writes=["angt"])
    kb.op('dve', lambda e: e.tensor_scalar(out=cosT[:], in0=angt[:], scalar1=1.0 / (2 * PI), scalar2=None, op0=ALU.mult),
          reads=["angt"], writes=["cosT"])
    kb.op('dve', lambda e: e.tensor_copy(out=kint[:], in_=cosT[:]), reads=["cosT"], writes=["kint"])
    kb.op('dve', lambda e: e.tensor_copy(out=cosT[:], in_=kint[:]), reads=["kint"], writes=["cosT"])
    kb.op('dve', lambda e: e.scalar_tensor_tensor(out=angt[:], in0=cosT[:], scalar=-2 * PI, in1=angt[:], op0=ALU.mult, op1=ALU.add),
          reads=["cosT", "angt"], writes=["angt"])
    kb.op('dve', lambda e: e.tensor_scalar(out=cosT[:], in0=angt[:], scalar1=PI, scalar2=-2 * PI, op0=ALU.is_gt, op1=ALU.mult),
          reads=["angt"], writes=["cosT"])
    kb.op('dve', lambda e: e.tensor_tensor(out=sinS[:], in0=angt[:], in1=cosT[:], op=ALU.add), reads=["angt", "cosT"], writes=["sinS"])
    kb.op('dve', lambda e: e.tensor_scalar(out=angt[:], in0=angt[:], scalar1=PI / 2, scalar2=None, op0=ALU.add),
          reads=["angt"], writes=["angt"])
    kb.op('dve', lambda e: e.tensor_scalar(out=cosT[:], in0=angt[:], scalar1=PI, scalar2=-2 * PI, op0=ALU.is_gt, op1=ALU.mult),
          reads=["angt"], writes=["cosT"])
    kb.op('dve', lambda e: e.tensor_tensor(out=cosT[:], in0=angt[:], in1=cosT[:], op=ALU.add), reads=["angt", "cosT"], writes=["cosT"])
    kb.op('act', lambda e: e.activation(out=sinS[:], in_=sinS[:], func=AF.Sin), reads=["sinS"], writes=["sinS"])
    kb.op('act', lambda e: e.activation(out=cosT[:], in_=cosT[:], func=AF.Sin), reads=["cosT"], writes=["cosT"])
    kb.op('dve', lambda e: e.tensor_scalar(out=sinS[:], in0=sinS[:], scalar1=SGN, scalar2=None, op0=ALU.mult),
          reads=["sinS", "CST"], writes=["sinS"])
    kb.op('act', lambda e: e.activation(out=sinkf[:], in_=sinkf[:], func=AF.Exp), reads=["sinkf"], writes=["sinkf"])
    kb.barrier()
    ar.reset(markA)
    qT = ar.get([128, 4, TOK], BF16)
    kTz = [ar.get([128, TOKH], BF16) for _ in range(2)]
    vT = ar.get([128, 9, 128], BF16)
    oT = ar.get([128, 4, TOK], BF16)
    tqs = [ar.get([128, 512], F32) for _ in range(2)]
    tas = [ar.get([128, 512], F32) for _ in range(2)]
    tbs = [ar.get([128, 512], F32) for _ in range(2)]
    Pb = [ar.get([128, 3, 512], BF16) for _ in range(2)]
    rden = ar.get([128, 512], F32)
    ev_i = [0]

    def qk_chunk(wt, wn, lc, bcol, toks, dsts):
        t0, t1 = toks
        n = t1 - t0
        i = ev_i[0] % 2
        bank = i
        ev_i[0] += 1
        tq, ta, tb_ = tqs[i], tas[i], tbs[i]
        tqn, tan, tbln, tbhn = "tq%d" % i, "ta%d" % i, "tbl%d" % i, "tbh%d" % i
        for k in range(16):
            kb.op('pe', lambda e: e.matmul(PS[bank][:, 0:n], lhsT=wt[:, k, lc * 128:(lc + 1) * 128], rhs=HT[:, k, t0:t1],
                                           start=(k == 0), stop=(k == 15)),
                  reads=HT_ALL + [wn], writes=[psname(bank)], inc=(k == 15))
        kb.op('act', lambda e: e.activation(out=tq[:, 0:n], in_=PS[bank][:, 0:n], func=AF.Identity, bias=bqk[:, bcol:bcol + 1]),
              reads=[psname(bank), "bqk"], writes=[tqn])
        kb.op('dve', lambda e: e.tensor_tensor(out=ta[:, 0:n], in0=tq[:, 0:n], in1=cosT[:, t0:t1], op=ALU.mult),
              reads=[tqn, "cosT"], writes=[tan])
        kb.op('pool', lambda e: e.tensor_tensor(out=tb_[0:64, 0:n], in0=tq[64:128, 0:n], in1=sinS[64:128, t0:t1], op=ALU.mult),
              reads=[tqn, "sinS"], writes=[tbln])
        kb.op('pool', lambda e: e.tensor_tensor(out=tb_[64:128, 0:n], in0=tq[0:64, 0:n], in1=sinS[0:64, t0:t1], op=ALU.mult),
              reads=[tqn, "sinS"], writes=[tbhn])
        if len(dsts) == 1:
            ap_, nm_, _ = dsts[0]
            kb.op('dve', lambda e: e.tensor_tensor(out=ap_, in0=ta[:, 0:n], in1=tb_[:, 0:n], op=ALU.add),
                  reads=[tan, tbln, tbhn], writes=[nm_])
        else:
            kb.op('dve', lambda e: e.tensor_tensor(out=ta[:, 0:n], in0=ta[:, 0:n], in1=tb_[:, 0:n], op=ALU.add),
                  reads=[tan, tbln, tbhn], writes=[tan])
            for (ap_, nm_, mcol) in dsts:
                kb.op('dve', lambda e: e.tensor_scalar(out=ap_, in0=ta[:, 0:n], scalar1=mcol, scalar2=None, op0=ALU.mult),
                      reads=[tan, "CST"], writes=[nm_])

    unit = 0
    qw = slots.load(kview(wqkv_in[:, 0:512]), (16, 512))
    kvw = slots.load(kview(wqkv_in[:, 512:768]), (16, 256))
    for m in range(4):
        cb = m * 768
        wt, wn = qw
        for lc in range(4):
            for th in range(2):
                qk_chunk(wt, wn, lc, 4 * m + lc, (th * 512, (th + 1) * 512), [(qT[:, lc, th * 512:(th + 1) * 512], "qT", None)])
        wow = load_wrows(awo_in[m * 512:(m + 1) * 512, :])
        wt, wn = kvw
        for (t0, t1) in [(0, 512), (512, 1024), (1024, 1152)]:
            qk_chunk(wt, wn, 0, 16 + m, (t0, t1), [(kTz[0][:, t0:t1], "kTz0", KM0), (kTz[1][:, t0:t1], "kTz1", KM1)])
        for t in range(9):
            bank = 2 + (t % 2)
            for k in range(16):
                kb.op('pe', lambda e: e.matmul(PS[bank][:, 0:128], lhsT=HT[:, k, t * 128:(t + 1) * 128], rhs=wt[:, k, 128:256],
                                               start=(k == 0), stop=False),
                      reads=HT_ALL + [wn], writes=[psname(bank)], inc=False)
            kb.op('pe', lambda e: e.matmul(PS[bank][:, 0:128], lhsT=ONESB[0:1, :], rhs=bv_bf[0:1, m * 128:(m + 1) * 128], start=False, stop=True),
                  reads=["ONESB", "bv_bf"], writes=[psname(bank)])
            kb.op('act', lambda e: e.activation(out=vT[:, t, :], in_=PS[bank][:, 0:128], func=AF.Copy), reads=[psname(bank)], writes=["vT"])
        if m < 3:
            qw = slots.load(kview(wqkv_in[:, cb + 768:cb + 768 + 512]), (16, 512))
        for e_ in range(2):
            h = 2 * m + e_
            kb.op('dve', lambda e: e.tensor_copy(out=sinkrows[e_][:], in_=sinkf[0:1, 4 * h:4 * h + 4].unsqueeze(2).to_broadcast([1, 4, 128])),
                  reads=["sinkf"], writes=["sinkrow%d" % e_])
        for n in range(8):
            for e_ in range(2):
                h = 2 * m + e_
                js = [j for j in range(3) if n + j - 1 >= 0]
                pb = Pb[unit % 2]
                pbn = "Pb%d" % (unit % 2)
                unit += 1
                for j in js:
                    kt = n + j - 1
                    bank = 2 + j
                    kb.op('pe', lambda e: e.matmul(PS[bank][:], lhsT=kTz[e_][:, kt * 128:(kt + 1) * 128],
                                                   rhs=qT[:, :, n * 128:(n + 1) * 128], start=True, stop=True),
                          reads=["qT", "kTz%d" % e_], writes=[psname(bank)])
                    kb.op('act', lambda e: e.activation(out=pb[:, j, :], in_=PS[bank][:], func=AF.Exp, scale=0.125),
                          reads=[psname(bank)], writes=[pbn + "_%d" % j])
                    if j == 0:
                        kb.op('pool', lambda e: e.tensor_tensor(out=pb[:, j, :], in0=pb[:, j, :], in1=MGE[:], op=ALU.mult),
                              reads=[pbn + "_%d" % j, "MGE"], writes=[pbn + "_%d" % j])
                    if j == 2:
                        kb.op('pool', lambda e: e.tensor_tensor(out=pb[:, j, :], in0=pb[:, j, :], in1=MLE[:], op=ALU.mult),
                              reads=[pbn + "_%d" % j, "MLE"], writes=[pbn + "_%d" % j])
                for idx, j in enumerate(js):
                    kt = n + j - 1
                    kb.op('pe', lambda e: e.matmul(PS[0][64 * e_:64 * e_ + 64, :], lhsT=vT[:, kt, e_ * 64:(e_ + 1) * 64], rhs=pb[:, j, :],
                                                   start=(idx == 0), stop=(idx == len(js) - 1)),
                          reads=["vT", pbn + "_%d" % j], writes=["ps0"], inc=(idx == len(js) - 1))
                for idx, j in enumerate(js):
                    kb.op('pe', lambda e: e.matmul(PS[1][64 * e_:64 * e_ + 64, :], lhsT=ONESB[:, 0:64], rhs=pb[:, j, :],
                                                   start=(idx == 0), stop=False),
                          reads=["ONESB", pbn + "_%d" % j], writes=["ps1"], inc=False)
                kb.op('pe', lambda e: e.matmul(PS[1][64 * e_:64 * e_ + 64, :], lhsT=ONESB[0:1, 0:64],
                                               rhs=sinkrows[e_][0:1, :, :], start=False, stop=True),
                      reads=["ONESB", "sinkrow%d" % e_], writes=["ps1"])
            kb.op('dve', lambda e: e.reciprocal(out=rden[:], in_=PS[1][:]), reads=["ps1"], writes=["rden"])
            kb.op('dve', lambda e: e.tensor_tensor(out=oT[:, :, n * 128:(n + 1) * 128],
                                                   in0=PS[0][:].rearrange("p (g q) -> p g q", g=4),
                                                   in1=rden[:].rearrange("p (g q) -> p g q", g=4), op=ALU.mult),
                  reads=["ps0", "rden"], writes=["oT"])
        proj_accum(lambda c: oT[:, c, :], ["oT"], wow[0], wow[1], banks=(4, 5, 6, 7))
        if m < 3:
            kvw = slots.load(kview(wqkv_in[:, cb + 768 + 512:cb + 768 + 768]), (16, 256))

    layer_norm(0)
    if stop_after == "attn":
        return finish()

    def mlp(layer, lidx):
        slots.i = 0
        w1w = slots.load(kview(w1_in[layer][:, 0:512]), (16, 512))
        w2w = load_wrows(w2_in[layer][0:512, :])
        modulate_prep(layer, 4, 3, 5, b2_in[layer], NT)
        slot3 = ar.get([128, SLOT_BYTES // 2], BF16)
        slots.raw.append(slot3)
        slots.n = 3
        slots.i = 2
        uT = [ar.get([128, 4, TOK], BF16) for _ in range(2)]
        b1c = ar.get([128, 64], F32)
        rl = [ar.get([128, 512], F32) for _ in range(2)]
        kb.dma('sp', b1c[:], b1_in[layer], writes=["b1c"])
        ai = 0
        for g in range(16):
            wt, wn = w1w
            u = uT[g % 2]
            un = "uT%d" % (g % 2)
            if g < 15:
                w1w = slots.load(kview(w1_in[layer][:, (g + 1) * 512:(g + 2) * 512]), (16, 512))
            for lc in range(4):
                for th in range(2):
                    bank = ai % 4
                    r_ = rl[ai % 2]
                    rn = "rl%d" % (ai % 2)
                    ai += 1
                    for k in range(16):
                        kb.op('pe', lambda e: e.matmul(PS[bank][:], lhsT=wt[:, k, lc * 128:(lc + 1) * 128], rhs=HT[:, k, th * 512:(th + 1) * 512],
                                                       start=(k == 0), stop=(k == 15)),
                              reads=HT_ALL + [wn], writes=[psname(bank)], inc=(k == 15))
                    kb.op('act', lambda e: e.activation(out=r_[:], in_=PS[bank][:], func=AF.Relu, bias=b1c[:, g * 4 + lc:g * 4 + lc + 1]),
                          reads=[psname(bank), "b1c"], writes=[rn])
                    kb.op('act', lambda e: e.activation(out=u[:, lc, th * 512:(th + 1) * 512], in_=r_[:], func=AF.Square),
                          reads=[rn], writes=[un])
            w2cur = w2w
            if g < 15:
                w2w = load_wrows(w2_in[layer][(g + 1) * 512:(g + 2) * 512, :])
            proj_accum(lambda c: u[:, c, :], [un], w2cur[0], w2cur[1], banks=(4, 5, 6, 7))
        kb.barrier()
        slots.raw.pop()
        slots.n = 2
        slots.i = 0
        layer_norm(lidx)

    mlp(0, 1)
    if stop_after == "mlp0":
        return finish()

    modulate_prep(1, 1, 0, 2, mbo_in, NT)
    wg = ar.get([128, 16, 16], BF16)
    gt = ar.get([128, NT, 16], F32)
    bg = ar.get([128, 16], F32)
    lf = ar.get([128, NT, 2, 4], F32)
    gg = ar.get([128, NT, 2, 4], F32)
    dd = ar.get([128, NT, 2, 4], F32)
    uu = ar.get([128, NT, 2, 4], F32)
    fl = ar.get([128, NT, 2, 4], F32)
    al = ar.get([128, 2, 4, NT], F32)
    mub = ar.get([128, 2, 4, NT], F32)
    mib = ar.get([128, 2, 4, NT], F32)
    Amax = ar.get([4, 2, NT], F32)
    Gsum = ar.get([4, 2, NT], F32)
    mu_s = ar.get([4, 2, NT], F32)
    mi_s = ar.get([4, 2, NT], F32)
    mA = ar.get([4, NT + 1], F32)
    mB = ar.get([4, NT + 1], F32)
    mrecv = ar.get([4, 2, 2], F32)
    mxs = ar.get([4, 2], F32)
    dg = ar.get([4, 2, 4, NT], F32)
    bmqk = ar.get([128, 16], F32)
    kb.dma('pool', wg[:], kview(mwg_in), writes=["wg"])
    bc_from_dram(bg[:], mbg_in, "bg")
    kb.dma('sp', bmqk[:], mbqk_in, writes=["bmqk"])
    for t in range(NT):
        for k in range(16):
            kb.op('pe', lambda e: e.matmul(PS[0][:, t * 16:(t + 1) * 16], lhsT=HT[:, k, t * 128:(t + 1) * 128], rhs=wg[:, k, :],
                                           start=(k == 0), stop=(k == 15)),
                  reads=HT_ALL + ["wg"], writes=["ps0"], inc=(k == 15))
    kb.op('dve', lambda e: e.tensor_tensor(out=gt[:], in0=PS[0][:, 0:128].rearrange("p (t c) -> p t c", t=NT),
                                           in1=bg[:].unsqueeze(1).to_broadcast([128, NT, 16]), op=ALU.add),
          reads=["ps0", "bg"], writes=["gt"])
    gt4 = gt[:].rearrange("p t (d i h) -> p t d i h", d=2, i=2)
    kb.op('act', lambda e: e.activation(out=lf[:], in_=gt4[:, :, :, 1, :], func=AF.Exp, scale=-1.0), reads=["gt"], writes=["lf"])
    kb.op('act', lambda e: e.activation(out=lf[:], in_=lf[:], func=AF.Ln, bias=1.0), reads=["lf"], writes=["lf"])
    kb.op('dve', lambda e: e.tensor_scalar(out=lf[:], in0=lf[:], scalar1=-1.0, scalar2=None, op0=ALU.mult), reads=["lf"], writes=["lf"])
    for t in range(NT):
        kb.op('pe', lambda e: e.matmul(PS[1][:, t * 8:t * 8 + 4], lhsT=MLEF, rhs=lf[:, t, 0, :], start=True, stop=True),
              reads=["CST", "lf"], writes=["ps1"], inc=False)
        kb.op('pe', lambda e: e.matmul(PS[1][:, t * 8 + 4:t * 8 + 8], lhsT=MGEF, rhs=lf[:, t, 1, :], start=True, stop=True),
              reads=["CST", "lf"], writes=["ps1"], inc=(t == NT - 1))
    kb.op('dve', lambda e: e.tensor_copy(out=gg[:], in_=PS[1][:, 0:64].rearrange("p (t d h) -> p t d h", t=NT, d=2)),
          reads=["ps1"], writes=["gg"])
    kb.op('dve', lambda e: e.tensor_tensor(out=dd[:], in0=gt4[:, :, :, 0, :], in1=gg[:], op=ALU.subtract), reads=["gt", "gg"], writes=["dd"])
    for d_ in range(2):
        for tq_ in range(2):
            bank = 2 + d_ * 2 + tq_
            for j in range(4):
                t = tq_ * 4 + j
                kb.op('pe', lambda e: e.matmul(PS[bank][0:4, j * 128:(j + 1) * 128], lhsT=dd[:, t, d_, :], rhs=IDF, start=True, stop=True),
                      reads=["dd", "CST"], writes=[psname(bank)], inc=(j == 3))
            kb.op('dve', lambda e: e.tensor_reduce(out=Amax[:, d_, tq_ * 4:(tq_ + 1) * 4],
                                                   in_=PS[bank][0:4, :].rearrange("p (j n) -> p j n", j=4), axis=AX.X, op=ALU.max),
                  reads=[psname(bank)], writes=["Amax"])
        for t in range(NT):
            kb.op('pe', lambda e: e.matmul(PS[6][0:4, d_ * NT + t:d_ * NT + t + 1], lhsT=lf[:, t, d_, :], rhs=ONESF[:, 0:1], start=True, stop=True),
                  reads=["lf", "ONESF"], writes=["ps6"], inc=(t == NT - 1))
    kb.op('dve', lambda e: e.tensor_copy(out=Gsum[:], in_=PS[6][0:4, 0:16].rearrange("p (d t) -> p d t", d=2)), reads=["ps6"], writes=["Gsum"])
    kb.op('dve', lambda e: e.memset(mA[:, 0:1], -1.0e30), writes=["mA"])
    for t in range(NT):
        kb.op('dve', lambda e: e.tensor_tensor(out=mu_s[:, 0, t:t + 1], in0=mA[:, t:t + 1], in1=Amax[:, 0, t:t + 1], op=ALU.max),
              reads=["mA", "Amax"], writes=["mu_s"])
        kb.op('dve', lambda e: e.tensor_tensor(out=mA[:, t + 1:t + 2], in0=Gsum[:, 0, t:t + 1], in1=mu_s[:, 0, t:t + 1], op=ALU.add),
              reads=["Gsum", "mu_s"], writes=["mA"])
    kb.op('dve', lambda e: e.tensor_copy(out=mi_s[:, 0, :], in_=mA[:, 0:NT]), reads=["mA"], writes=["mi_s"])
    kb.op('dve', lambda e: e.memset(mxs[:], 0.0), writes=["mxs"])
    kb.op('dve', lambda e: e.tensor_copy(out=mxs[:, 0:1], in_=mA[:, NT:NT + 1]), reads=["mA", "mxs"], writes=["mxs"])
    kb.dma('sp', mx_in.ap()[0:1, :].rearrange("o (h j) -> (o h) j", j=2), mxs[:], reads=["mxs"], writes=["mx_in"])
    collective(PAIRS, mx_in, mx_out, ["mx_in"], ["mx_out"])
    with nc.allow_non_contiguous_dma(reason="tiny scalar exchange"):
        kb.dma('sp', mrecv[:], mx_out.ap()[0:2, :].rearrange("s (h j) -> h s j", j=2), reads=["mx_out"], writes=["mrecv"])
    kb.op('dve', lambda e: e.tensor_scalar(out=mB[:, NT:NT + 1], in0=mrecv[:, 0, 0:1], scalar1=SEL[0:4, 0:1], scalar2=None, op0=ALU.mult),
          reads=["mrecv", "SEL"], writes=["mB"])
    kb.op('dve', lambda e: e.scalar_tensor_tensor(out=mB[:, NT:NT + 1], in0=mrecv[:, 1, 0:1], scalar=SEL[0:4, 1:2], in1=mB[:, NT:NT + 1],
                                                  op0=ALU.mult, op1=ALU.add), reads=["mrecv", "SEL", "mB"], writes=["mB"])
    for t in range(NT - 1, -1, -1):
        kb.op('dve', lambda e: e.tensor_tensor(out=mu_s[:, 1, t:t + 1], in0=mB[:, t + 1:t + 2], in1=Amax[:, 1, t:t + 1], op=ALU.max),
              reads=["mB", "Amax"], writes=["mu_s"])
        kb.op('dve', lambda e: e.tensor_tensor(out=mB[:, t:t + 1], in0=Gsum[:, 1, t:t + 1], in1=mu_s[:, 1, t:t + 1], op=ALU.add),
              reads=["Gsum", "mu_s"], writes=["mB"])
    kb.op('dve', lambda e: e.tensor_copy(out=mi_s[:, 1, :], in_=mB[:, 1:NT + 1]), reads=["mB"], writes=["mi_s"])
    for (src, srcn, dst, dstn) in [(mu_s, "mu_s", mub, "mub"), (mi_s, "mi_s", mib, "mib")]:
        kb.op('dve', lambda e: e.tensor_tensor(out=dg[:], in0=src[:].unsqueeze(2).to_broadcast([4, 2, 4, NT]),
                                               in1=EYE4.unsqueeze(1).unsqueeze(3).to_broadcast([4, 2, 4, NT]), op=ALU.mult),
              reads=[srcn, "CST"], writes=["dg"])
        kb.op('pe', lambda e: e.matmul(PS[7][:, 0:64], lhsT=ONESF[0:4, :], rhs=dg[:].rearrange("p d h t -> p (d h t)"), start=True, stop=True),
              reads=["ONESF", "dg"], writes=["ps7"])
        kb.op('dve', lambda e: e.tensor_copy(out=dst[:].rearrange("p d h t -> p (d h t)"), in_=PS[7][:, 0:64]), reads=["ps7"], writes=[dstn])
    kb.op('dve', lambda e: e.tensor_tensor(out=al[:], in0=mib[:], in1=mub[:], op=ALU.subtract), reads=["mib", "mub"], writes=["al"])
    kb.op('act', lambda e: e.activation(out=al[:], in_=al[:], func=AF.Exp), reads=["al"], writes=["al"])
    mubT = mub[:].rearrange("p d h t -> p t d h")
    kb.op('dve', lambda e: e.tensor_tensor(out=uu[:], in0=dd[:], in1=mubT, op=ALU.subtract), reads=["dd", "mub"], writes=["uu"])
    kb.op('act', lambda e: e.activation(out=uu[:], in_=uu[:], func=AF.Exp), reads=["uu"], writes=["uu"])
    kb.op('dve', lambda e: e.scalar_tensor_tensor(out=fl[:], in0=gg[:], scalar=-1.0, in1=mubT, op0=ALU.mult, op1=ALU.subtract),
          reads=["gg", "mub"], writes=["fl"])
    kb.op('act', lambda e: e.activation(out=fl[:], in_=fl[:], func=AF.Exp), reads=["fl"], writes=["fl"])

    if stop_after == "gates":
        for i_, (src_, n_) in enumerate([(uu, "uu"), (fl, "fl"), (al, "al"), (gg, "gg"), (dd, "dd")]):
            kb.op('dve', lambda e: e.tensor_copy(out=X[:, 0, i_ * 64:(i_ + 1) * 64], in_=src_[:].rearrange("p a b c -> p (a b c)")),
                  reads=[n_, "X0"], writes=["X0"])
        return finish()

    qTh = ar.get([128, 2, TOK], BF16)
    kTh = ar.get([128, 2, TOK], BF16)
    vh = ar.get([128, NT, 512], BF16)
    og = ar.get([128, NT, 512], BF16)
    hA = ar.get([128, NT, 512], BF16)
    Cst = ar.get([128, 2, 512], F32)
    Csh0 = ar.get([128, 2, 512], BF16)
    nn = ar.get([128, 2], F32)
    nshB = [ar.get([128, 2], BF16) for _ in range(2)]
    ntmp = ar.get([128, 2], F32)
    hs = ar.get([128, 512], F32)
    hn = ar.get([128, 512], F32)
    ctmp = ar.get([128, 2, 512], F32)
    Csh1 = ctmp[:, 0, :].bitcast(BF16).rearrange("p (c n) -> p c n", c=2)
    CshB = [Csh0[:], Csh1]
    CshN = ["Csh0", "ctmp"]
    kp = [ar.get([128, 256], BF16) for _ in range(2)]
    scT = [ar.get([128, 128], BF16) for _ in range(2)]
    yb = ar.get([128, 512], BF16)
    yT = ar.get([128, 4, 128], BF16)
    bvo = ar.get([1, 1024], BF16)
    dn = ar.get([128, 1], F32)
    rr = ar.get([128, 1], F32)
    st6 = ar.get([128, 6], F32)
    mv2 = ar.get([128, 2], F32)
    rstd2 = ar.get([128, 1], F32)
    nb2 = ar.get([128, 1], F32)
    pst5 = PS[5][:].bitcast(BF16)
    kb.op('pool', lambda e: e.memset(hs[:], 0.0), writes=["hs"])
    for hd_ in range(4):
        kb.dma('sp', st_in[hd_].ap()[256:264, :], hs[0:8, :], reads=["hs"], writes=["st_in%dn" % hd_])
    ui = [0]

    def run_pass(Dn, tiles, hd, wo_t, wo_n):
        mask = MLEF if Dn == 0 else MGEF
        for idx, t in enumerate(tiles):
            nxt = tiles[idx + 1] if idx + 1 < len(tiles) else None
            ucol = uu[:, t, Dn, hd:hd + 1]
            flcol = fl[:, t, Dn, hd:hd + 1]
            alcol = al[:, Dn, hd, t:t + 1]
            i2 = ui[0] % 2
            ui[0] += 1
            kpt, kpn = kp[i2], "kp%d" % i2
            sct, scn = scT[i2], "scT%d" % i2
            cs_cur, cs_curn = CshB[idx % 2], CshN[idx % 2]
            cs_nxt, cs_nxtn = CshB[(idx + 1) % 2], CshN[(idx + 1) % 2]
            ns_cur, ns_curn = nshB[idx % 2], "nsh%d" % (idx % 2)
            ns_nxt, ns_nxtn = nshB[(idx + 1) % 2], "nsh%d" % ((idx + 1) % 2)
            tsl = slice(t * 128, (t + 1) * 128)
            for dc in range(2):
                kb.op('pe', lambda e: e.transpose(out=pst5[:, dc * 128:(dc + 1) * 128], in_=kTh[:, dc, tsl], identity=IDB[:]),
                      reads=["kTh", "IDB"], writes=["ps5"], inc=(dc == 1))
            for dc in range(2):
                kb.op('pe', lambda e: e.matmul(PS[0][:, 0:128], lhsT=kTh[:, dc, tsl], rhs=qTh[:, dc, tsl], start=(dc == 0), stop=(dc == 1)),
                      reads=["kTh", "qTh"], writes=["ps0"], inc=(dc == 1))
            kb.op('act', lambda e: e.activation(out=kpt[:], in_=pst5[:, 0:256], func=AF.Copy, scale=ucol),
                  reads=["ps5", "uu"], writes=[kpn])
            kb.op('dve', lambda e: e.scalar_tensor_tensor(out=sct[:], in0=PS[0][:, 0:128], scalar=ucol, in1=mask, op0=ALU.mult, op1=ALU.mult),
                  reads=["ps0", "uu", "CST"], writes=[scn])
            need_state = (nxt is not None) or (Dn == 0)
            if need_state:
                for dc in range(2):
                    kb.op('pe', lambda e: e.matmul(PS[3 + dc][:], lhsT=kpt[:, dc * 128:(dc + 1) * 128], rhs=vh[:, t, :], start=True, stop=True),
                          reads=[kpn, "vh"], writes=[psname(3 + dc)])
                    kb.op('pe', lambda e: e.matmul(PS[0][:, 128 + dc:129 + dc], lhsT=kpt[:, dc * 128:(dc + 1) * 128], rhs=ONESB[:, 0:1], start=True, stop=True),
                          reads=[kpn, "ONESB"], writes=["ps0"])
                for dc in range(2):
                    kb.op('dve', lambda e: e.scalar_tensor_tensor(out=Cst[:, dc, :], in0=Cst[:, dc, :], scalar=alcol, in1=PS[3 + dc][:],
                                                                  op0=ALU.mult, op1=ALU.add),
                          reads=["Cst", "al", psname(3 + dc)], writes=["Cst"])
                kb.op('dve', lambda e: e.scalar_tensor_tensor(out=nn[:], in0=nn[:], scalar=alcol, in1=PS[0][:, 128:130], op0=ALU.mult, op1=ALU.add),
                      reads=["nn", "al", "ps0"], writes=["nn"])
                if nxt is not None:
                    alnx = al[:, Dn, hd, nxt:nxt + 1]
                    kb.op('act', lambda e: e.activation(out=cs_nxt, in_=Cst[:], func=AF.Copy, scale=alnx), reads=["Cst", "al"], writes=[cs_nxtn])
                    kb.op('act', lambda e: e.activation(out=ns_nxt[:], in_=nn[:], func=AF.Copy, scale=alnx), reads=["nn", "al"], writes=[ns_nxtn])
            for dc in range(2):
                kb.op('pe', lambda e: e.matmul(PS[1][:], lhsT=qTh[:, dc, tsl], rhs=cs_cur[:, dc, :], start=(dc == 0), stop=False),
                      reads=["qTh", cs_curn], writes=["ps1"], inc=False)
            kb.op('pe', lambda e: e.matmul(PS[1][:], lhsT=sct[:], rhs=vh[:, t, :], start=False, stop=True),
                  reads=[scn, "vh"], writes=["ps1"])
            for dc in range(2):
                kb.op('pe', lambda e: e.matmul(PS[2][:, 0:1], lhsT=qTh[:, dc, tsl], rhs=ns_cur[:, dc:dc + 1], start=(dc == 0), stop=False),
                      reads=["qTh", ns_curn], writes=["ps2"], inc=False)
            kb.op('pe', lambda e: e.matmul(PS[2][:, 0:1], lhsT=sct[:], rhs=ONESB[:, 0:1], start=False, stop=True),
                  reads=[scn, "ONESB"], writes=["ps2"])
            kb.op('dve', lambda e: e.tensor_scalar(out=rr[:], in0=PS[2][:, 0:1], scalar1=-1.0, scalar2=flcol, op0=ALU.mult, op1=ALU.max),
                  reads=["ps2", "fl"], writes=["rr"])
            kb.op('dve', lambda e: e.tensor_scalar(out=dn[:], in0=PS[2][:, 0:1], scalar1=rr[:], scalar2=None, op0=ALU.max),
                  reads=["ps2", "rr"], writes=["dn"])
            kb.op('dve', lambda e: e.reciprocal(out=rr[:], in_=dn[:]), reads=["dn"], writes=["rr"])
            if Dn == 0:
                kb.op('act', lambda e: e.activation(out=hA[:, t, :], in_=PS[1][:], func=AF.Copy, scale=rr[:]),
                      reads=["ps1", "rr"], writes=["hA"])
            else:
                kb.op('dve', lambda e: e.scalar_tensor_tensor(out=hs[:], in0=PS[1][:], scalar=rr[:], in1=hA[:, t, :], op0=ALU.mult, op1=ALU.add),
                      reads=["ps1", "rr", "hA"], writes=["hs"])
            if Dn == 1:
                kb.op('dve', lambda e: e.bn_stats(out=st6[:], in_=hs[:]), reads=["hs"], writes=["st6"])
                kb.op('dve', lambda e: e.bn_aggr(out=mv2[:], in_=st6[:]), reads=["st6"], writes=["mv2"])
                kb.op('dve', lambda e: e.tensor_scalar(out=rstd2[:], in0=mv2[:, 1:2], scalar1=HN_EPS, scalar2=None, op0=ALU.add),
                      reads=["mv2"], writes=["rstd2"])
                kb.op('act', lambda e: e.activation(out=rstd2[:], in_=rstd2[:], func=AF.Sqrt), reads=["rstd2"], writes=["rstd2"])
                kb.op('dve', lambda e: e.reciprocal(out=rstd2[:], in_=rstd2[:]), reads=["rstd2"], writes=["rstd2"])
                kb.op('dve', lambda e: e.scalar_tensor_tensor(out=nb2[:], in0=mv2[:, 0:1], scalar=-1.0, in1=rstd2[:], op0=ALU.mult, op1=ALU.mult),
                      reads=["mv2", "rstd2"], writes=["nb2"])
                kb.op('act', lambda e: e.activation(out=hn[:], in_=hs[:], func=AF.Identity, bias=nb2[:], scale=rstd2[:]),
                      reads=["hs", "nb2", "rstd2"], writes=["hn"])
                kb.op('dve', lambda e: e.tensor_tensor(out=yb[:], in0=hn[:], in1=og[:, t, :], op=ALU.mult), reads=["hn", "og"], writes=["yb"])
                for c in range(4):
                    kb.op('pe', lambda e: e.transpose(out=pst5[:, 256 + c * 128:256 + (c + 1) * 128], in_=yb[:, c * 128:(c + 1) * 128], identity=IDB[:]),
                          reads=["yb", "IDB"], writes=["ps5"], inc=(c == 3))
                kb.op('act', lambda e: e.activation(out=yT[:], in_=pst5[:, 256:768].rearrange("p (c n) -> p c n", c=4), func=AF.Copy),
                      reads=["ps5"], writes=["yT"])
                for cc in range(4):
                    def mm(bank):
                        for c in range(4):
                            kb.op('pe', lambda e: e.matmul(PS[bank][:], lhsT=yT[:, c, :], rhs=wo_t[:, c, cc * 512:(cc + 1) * 512],
                                                           start=(c == 0), stop=(c == 3)),
                                  reads=["yT", wo_n], writes=[psname(bank)], inc=(c == 3))
                    accum_block(t, cc, mm, banks=(6, 7))

    slots.i = 0
    qkw = slots.load(kview(mwin_in[:, 0:512]), (16, 512))
    for hd in range(4):
        cb = hd * 1536
        kb.dma('pool', bvo[:], mbvo_in[0:1, hd * 1024:(hd + 1) * 1024], writes=["bvo"])
        bc_from_dram(hn[:], mnw_in[0:1, hd * 512:(hd + 1) * 512], "hn")
        wt, wn = qkw
        vw = slots.load(kview(mwin_in[:, cb + 512:cb + 1024]), (16, 512))
        pi = 0
        for dc in range(2):
            for th in range(2):
                bank = pi % 2
                pi += 1
                for k in range(16):
                    kb.op('pe', lambda e: e.matmul(PS[bank][:], lhsT=wt[:, k, dc * 128:(dc + 1) * 128], rhs=HT[:, k, th * 512:(th + 1) * 512],
                                                   start=(k == 0), stop=(k == 15)),
                          reads=HT_ALL + [wn], writes=[psname(bank)], inc=(k == 15))
                kb.op('dve', lambda e: e.tensor_scalar(out=qTh[:, dc, th * 512:(th + 1) * 512], in0=PS[bank][:],
                                                       scalar1=bmqk[:, hd * 4 + dc:hd * 4 + dc + 1], scalar2=0.0625, op0=ALU.add, op1=ALU.mult),
                      reads=[psname(bank), "bmqk"], writes=["qTh"])
        for dc in range(2):
            for th in range(2):
                bank = pi % 2
                pi += 1
                for k in range(16):
                    kb.op('pe', lambda e: e.matmul(PS[bank][:], lhsT=wt[:, k, 256 + dc * 128:256 + (dc + 1) * 128], rhs=HT[:, k, th * 512:(th + 1) * 512],
                                                   start=(k == 0), stop=(k == 15)),
                          reads=HT_ALL + [wn], writes=[psname(bank)], inc=(k == 15))
                kb.op('act', lambda e: e.activation(out=kTh[:, dc, th * 512:(th + 1) * 512], in_=PS[bank][:], func=AF.Identity,
                                                    bias=bmqk[:, hd * 4 + 2 + dc:hd * 4 + 3 + dc]),
                      reads=[psname(bank), "bmqk"], writes=["kTh"])
        ow = slots.load(kview(mwin_in[:, cb + 1024:cb + 1536]), (16, 512))
        wt, wn = vw
        for t in range(NT):
            bank = 2 + (t % 2)
            for k in range(16):
                kb.op('pe', lambda e: e.matmul(PS[bank][:], lhsT=HT[:, k, t * 128:(t + 1) * 128], rhs=wt[:, k, :], start=(k == 0), stop=False),
                      reads=HT_ALL + [wn], writes=[psname(bank)], inc=False)
            kb.op('pe', lambda e: e.matmul(PS[bank][:], lhsT=ONESB[0:1, :], rhs=bvo[0:1, 0:512], start=False, stop=True),
                  reads=["ONESB", "bvo"], writes=[psname(bank)])
            kb.op('act', lambda e: e.activation(out=vh[:, t, :], in_=PS[bank][:], func=AF.Copy), reads=[psname(bank)], writes=["vh"])
        wo_t, wo_n = load_wrows(mwo_in[hd * 512:(hd + 1) * 512, :])
        wt, wn = ow
        for t in range(NT):
            bank = 2 + (t % 2)
            for k in range(16):
                kb.op('pe', lambda e: e.matmul(PS[bank][:], lhsT=HT[:, k, t * 128:(t + 1) * 128], rhs=wt[:, k, :], start=(k == 0), stop=False),
                      reads=HT_ALL + [wn], writes=[psname(bank)], inc=False)
            kb.op('pe', lambda e: e.matmul(PS[bank][:], lhsT=ONESB[0:1, :], rhs=bvo[0:1, 512:1024], start=False, stop=True),
                  reads=["ONESB", "bvo"], writes=[psname(bank)])
            kb.op('act', lambda e: e.activation(out=hs[:], in_=PS[bank][:], func=AF.Sigmoid), reads=[psname(bank)], writes=["hs"])
            kb.op('dve', lambda e: e.tensor_tensor(out=og[:, t, :], in0=hs[:], in1=hn[:], op=ALU.mult), reads=["hs", "hn"], writes=["og"])
        if hd < 3:
            qkw = slots.load(kview(mwin_in[:, cb + 1536:cb + 1536 + 512]), (16, 512))
        gate_fold(wo_t, wo_n)
        kb.op('pool', lambda e: e.memset(Cst[:], 0.0), writes=["Cst"])
        kb.op('pool', lambda e: e.memset(nn[:], 0.0), writes=["nn"])
        kb.op('pool', lambda e: e.memset(CshB[0], 0.0), writes=[CshN[0]])
        kb.op('pool', lambda e: e.memset(nshB[0][:], 0.0), writes=["nsh0"])
        run_pass(0, list(range(NT)), hd, None, None)
        stn = "st_in%d" % hd
        son = "st_out%d" % hd
        kb.dma('sp', st_in[hd].ap()[0:256, :].rearrange("(c p) n -> p c n", p=128), Cst[:], reads=["Cst"], writes=[stn])
        with nc.allow_non_contiguous_dma(reason="tiny n vector"):
            kb.dma('sp', st_in[hd].ap()[256:258, 0:128].rearrange("c p -> p c"), nn[:], reads=["nn"], writes=[stn + "n"])
        collective(PAIRS, st_in[hd], st_out[hd], [stn, stn + "n"], [son])
        for s_ in range(2):
            kb.dma('sp', ctmp[:], st_out[hd].ap()[s_ * 264:s_ * 264 + 256, :].rearrange("(c p) n -> p c n", p=128), reads=[son], writes=["ctmp"])
            with nc.allow_non_contiguous_dma(reason="tiny n vector"):
                kb.dma('sp', ntmp[:], st_out[hd].ap()[s_ * 264 + 256:s_ * 264 + 258, 0:128].rearrange("c p -> p c"), reads=[son], writes=["ntmp"])
            if s_ == 0:
                kb.op('dve', lambda e: e.tensor_scalar(out=Cst[:], in0=ctmp[:], scalar1=SEL[:, 0:1], scalar2=None, op0=ALU.mult),
                      reads=["ctmp", "SEL"], writes=["Cst"])
                kb.op('dve', lambda e: e.tensor_scalar(out=nn[:], in0=ntmp[:], scalar1=SEL[:, 0:1], scalar2=None, op0=ALU.mult),
                      reads=["ntmp", "SEL"], writes=["nn"])
            else:
                kb.op('dve', lambda e: e.scalar_tensor_tensor(out=Cst[:], in0=ctmp[:], scalar=SEL[:, 1:2], in1=Cst[:], op0=ALU.mult, op1=ALU.add),
                      reads=["ctmp", "SEL", "Cst"], writes=["Cst"])
                kb.op('dve', lambda e: e.scalar_tensor_tensor(out=nn[:], in0=ntmp[:], scalar=SEL[:, 1:2], in1=nn[:], op0=ALU.mult, op1=ALU.add),
                      reads=["ntmp", "SEL", "nn"], writes=["nn"])
        al0 = al[:, 1, hd, NT - 1:NT]
        kb.op('act', lambda e: e.activation(out=CshB[0], in_=Cst[:], func=AF.Copy, scale=al0), reads=["Cst", "al"], writes=[CshN[0]])
        kb.op('act', lambda e: e.activation(out=nshB[0][:], in_=nn[:], func=AF.Copy, scale=al0), reads=["nn", "al"], writes=["nsh0"])
        run_pass(1, list(range(NT - 1, -1, -1)), hd, wo_t, wo_n)

    layer_norm(2)
    if stop_after == "mlstm":
        return finish()
    mlp(1, 3)
    return finish()


def _consts():
    cst = np.zeros((128, 1024), np.float32)
    p = np.arange(128)
    cst[:, 0:128] = np.eye(128, dtype=np.float32)
    cst[:, 128:256] = (p[:, None] >= p[None, :]).astype(np.float32)
    cst[:, 256:384] = (p[:, None] <= p[None, :]).astype(np.float32)
    inv_freq = 1.0 / (10000.0 ** (np.arange(0, 64, 2, dtype=np.float32) / 64.0))
    cst[:, 384] = inv_freq[p % 32]
    cst[:, 385] = np.where(p >= 64, -1.0, 1.0)
    km0 = ((p // 32) % 2 == 0).astype(np.float32)
    cst[:, 386] = km0
    cst[:, 387] = 1.0 - km0
    cst[0:4, 388:392] = np.eye(4, dtype=np.float32)
    return cst


def _qk_perm():
    cols = []
    for c in range(16):
        m, g = c // 4, c % 4
        ha, hb = (2 * m) * 4 + g, (2 * m + 1) * 4 + g
        cols += [ha * 64 + d for d in range(32)] + [hb * 64 + d for d in range(32)]
        cols += [ha * 64 + 32 + d for d in range(32)] + [hb * 64 + 32 + d for d in range(32)]
    for m in range(4):
        ha, hb = 2 * m, 2 * m + 1
        base = 2048
        cols += [base + ha * 64 + d for d in range(32)] + [base + hb * 64 + d for d in range(32)]
        cols += [base + ha * 64 + 32 + d for d in range(32)] + [base + hb * 64 + 32 + d for d in range(32)]
    return np.array(cols)


def _wo_perm():
    rows = []
    for c in range(16):
        m, g = c // 4, c % 4
        for e in range(2):
            hq = (2 * m + e) * 4 + g
            rows += [hq * 64 + d for d in range(64)]
    return np.array(rows)


def _mlstm_cols(hf):
    cols = []
    for hd in range(4):
        cols += list(range(hd * 256, (hd + 1) * 256))
        cols += list(range(1024 + hd * 256, 1024 + (hd + 1) * 256))
        cols += list(range(2048 + hd * 512, 2048 + (hd + 1) * 512))
        cols += list(range(4096 + hd * 512, 4096 + (hd + 1) * 512))
    g0 = 6144
    if hf == 0:
        gcols = [g0 + i for i in range(16)]
    else:
        gcols = [g0 + 8 + i for i in range(8)] + [g0 + i for i in range(8)]
    return np.array(cols), np.array(gcols)


def prep_inputs(inputs):
    f = lambda a: np.ascontiguousarray(np.asarray(a))
    x = f(inputs["x"]); c = f(inputs["c"]); pos = f(inputs["positions"])
    cst = _consts()
    qkp = _qk_perm()
    wop = _wo_perm()
    wqkv = f(inputs["attn_w_qkv"])[0]
    bqkv = f(inputs["attn_b_qkv"])[0]
    parts = []
    for m in range(4):
        parts.append(wqkv[:, qkp[m * 512:(m + 1) * 512]])
        parts.append(wqkv[:, qkp[2048 + m * 128:2048 + (m + 1) * 128]])
        parts.append(wqkv[:, 2560 + m * 128:2560 + (m + 1) * 128])
    wqkv_p = np.ascontiguousarray(np.concatenate(parts, axis=1))
    bqk = np.ascontiguousarray(bqkv[qkp].reshape(20, 128).T)
    bv = np.ascontiguousarray(bqkv[2560:3072][None, :])
    sink = f(inputs["attn_sink"])[0][None, :]
    awo = np.ascontiguousarray(f(inputs["attn_w_o"])[0][wop, :])
    abo = f(inputs["attn_b_o"])[0][None, :]
    mw = f(inputs["mlstm_w_in"])[0]
    mb = f(inputs["mlstm_b_in"])[0]
    mnw = f(inputs["mlstm_norm_w"])[0][None, :]
    mwo = f(inputs["mlstm_w_o"])[0]
    mbo = f(inputs["mlstm_b_o"])[0][None, :]
    mod_w = f(inputs["mod_w"]); mod_b = f(inputs["mod_b"])
    w1 = f(inputs["mlp_w1"]); b1 = f(inputs["mlp_b1"]); w2 = f(inputs["mlp_w2"]); b2 = f(inputs["mlp_b2"])
    b1c = np.ascontiguousarray(b1.reshape(2, 64, 128).transpose(0, 2, 1))
    b2r = np.ascontiguousarray(b2[:, None, :])
    lng = np.ascontiguousarray(np.stack([inputs["ln_mix_g"][0], inputs["ln_mlp_g"][0], inputs["ln_mix_g"][1], inputs["ln_mlp_g"][1]])[:, None, :]).astype(np.float32)
    lnb = np.ascontiguousarray(np.stack([inputs["ln_mix_b"][0], inputs["ln_mlp_b"][0], inputs["ln_mix_b"][1], inputs["ln_mlp_b"][1]])[:, None, :]).astype(np.float32)
    cT = np.ascontiguousarray(c.reshape(4, 16, 128).transpose(2, 1, 0))
    mlstm_variants = {}
    for hf in range(2):
        cols, gcols = _mlstm_cols(hf)
        mwin = np.ascontiguousarray(mw[:, cols])
        mwg = np.ascontiguousarray(mw[:, gcols])
        bq = mb[cols].reshape(4, 1536)
        mbqk = np.ascontiguousarray(bq[:, 0:512].reshape(4, 4, 128).transpose(2, 0, 1).reshape(128, 16))
        mbvo = np.ascontiguousarray(bq[:, 512:1536].reshape(1, 4096))
        mbg = np.ascontiguousarray(mb[gcols][None, :])
        mlstm_variants[hf] = (mwin, mwg, mbqk, mbvo, mbg)
    in_maps = []
    for r in range(8):
        b, hf = r // 2, r % 2
        idx = np.arange(TOKH) if hf == 0 else (2047 - np.arange(TOKH))
        x_loc = np.ascontiguousarray(x[b, idx, :])
        pos_loc = np.ascontiguousarray(pos[b, idx][None, :]).astype(np.int32)
        modw = np.ascontiguousarray(mod_w.reshape(2, D, 6, 8, 256)[:, :, :, r, :].transpose(1, 0, 2, 3).reshape(D, 3072))
        modb = np.ascontiguousarray(mod_b.reshape(2, 6, 8, 256)[:, :, r, :].reshape(1, 3072))
        onehot = np.zeros((4, 128), np.float32); onehot[b, :] = 1.0
        sel = np.zeros((128, 2), np.float32); sel[:, 1 - hf] = 1.0
        mwin, mwg, mbqk, mbvo, mbg = mlstm_variants[hf]
        in_maps.append({
            "x_loc": x_loc, "pos_loc": pos_loc, "cT": cT, "modw": modw, "modb": modb, "onehot": onehot, "sel": sel,
            "cst": cst, "wqkv": wqkv_p, "bqk": bqk, "bv": bv, "sink": sink, "awo": awo, "abo": abo,
            "mwin": mwin, "mwg": mwg, "mbqk": mbqk, "mbvo": mbvo, "mbg": mbg, "mnw": mnw, "mwo": mwo, "mbo": mbo,
            "w1": w1, "b1c": b1c, "w2": w2, "b2": b2r, "lng": lng, "lnb": lnb,
        })
    return in_maps


def assemble(results):
    out = np.zeros((4, 2048, D), np.float32)
    for r in range(8):
        b, hf = r // 2, r % 2
        y = np.asarray(results[r]["y_out"])
        if hf == 0:
            out[b, 0:1024] = y
        else:
            out[b, 1024:2048] = y[::-1]
    return out


_NC_CACHE = {}


def kernel(**inputs):
    in_maps = prep_inputs(inputs)
    if "nc" not in _NC_CACHE:
        _NC_CACHE["nc"] = build()
    res = run_bass_kernel_spmd(_NC_CACHE["nc"], in_maps, core_ids=list(range(8)))
    return assemble(res.results)
```
